# Optimizing a Trainium2 kernel written in Bass

```python
import jax, jax.numpy as jnp
from jax import lax
import numpy as np

D_MODEL = 1024
BATCH = 8
SEQ = 2048
DEPTH = 4
DEC_BATCH = 128
DEC_SEQ = 8
PAST_LEN = 8192
PAGE_SIZE = 128

N_MIXERS = 2
N_CONV_LAYERS = (DEPTH + 1) // 2
N_MLA_LAYERS = DEPTH // 2
CONV_W = 31
MLA_HEADS = 16
Q_LORA = 384
KV_LORA = 256
QK_NOPE = 64
QK_ROPE = 32
V_HEAD = 64
MLA_SCALE = (QK_NOPE + QK_ROPE) ** -0.5
ROPE_THETA = 10000.0
Q_BLOCK = 128
N_MEM = 256
XA_HEADS = 4
XA_HEAD_DIM = 128
XA_WIDTH = XA_HEADS * XA_HEAD_DIM
D_FF = 2816
FFN_RESID = 0.5
RMS_EPS = 1e-6
LN_EPS = 1e-5
NEG_INF = -1e30

kernel_name = "hybrid_conformer_mla_decoder_step"


def rms_norm(x, g):
    xf = x.astype(jnp.float32)
    y = xf * lax.rsqrt(jnp.mean(xf * xf, axis=-1, keepdims=True) + RMS_EPS)
    return (y * g.astype(jnp.float32)).astype(x.dtype)


def layer_norm(x, g, b):
    xf = x.astype(jnp.float32)
    mu = jnp.mean(xf, axis=-1, keepdims=True)
    var = jnp.mean(jnp.square(xf - mu), axis=-1, keepdims=True)
    y = (xf - mu) * lax.rsqrt(var + LN_EPS) * g.astype(jnp.float32) + b.astype(jnp.float32)
    return y.astype(x.dtype)


def swiglu_ffn(x, w_in, w_out):
    gate, up = jnp.split(x @ w_in, 2, axis=-1)
    return (jax.nn.silu(gate) * up) @ w_out


def rope_angles(pos):
    inv_freq = ROPE_THETA ** (-jnp.arange(0, QK_ROPE, 2, dtype=jnp.float32) / QK_ROPE)
    ang = pos.astype(jnp.float32)[:, None] * inv_freq[None, :]
    return jnp.cos(ang), jnp.sin(ang)


def apply_rope(x, cos, sin):
    xf = x.astype(jnp.float32)
    x1, x2 = jnp.split(xf, 2, axis=-1)
    out = jnp.concatenate([x1 * cos - x2 * sin, x2 * cos + x1 * sin], axis=-1)
    return out.astype(x.dtype)


def conv_module(x, prev, w_pw1, b_pw1, w_dw, b_dw, ln_g, ln_b, w_pw2, b_pw2):
    a, gate = jnp.split(x @ w_pw1 + b_pw1, 2, axis=-1)
    u = a * jax.nn.sigmoid(gate)
    ext = jnp.concatenate([prev, u], axis=1)
    y = lax.conv_general_dilated(
        ext, w_dw[:, None, :], window_strides=(1,), padding="VALID",
        dimension_numbers=("NWC", "WIO", "NWC"), feature_group_count=ext.shape[-1])
    y = y + b_dw
    y = jax.nn.silu(layer_norm(y, ln_g, ln_b))
    return y @ w_pw2 + b_pw2, ext[:, -(CONV_W - 1):]


def latent_attention(q_lat, q_pe, k_lat, k_pe, q_pos):
    b, q, h, l = q_lat.shape
    r = q_pe.shape[-1]
    k_pos = jnp.arange(k_lat.shape[1], dtype=jnp.int32)
    blk = Q_BLOCK if q % Q_BLOCK == 0 else q
    nb = q // blk

    def one_block(args):
        ql, qp, pos = args
        s = (jnp.einsum("bqhl,bkl->bhqk", ql, k_lat) +
             jnp.einsum("bqhr,bkr->bhqk", qp, k_pe)).astype(jnp.float32) * MLA_SCALE
        mask = k_pos[None, :] <= pos[:, None]
        s = jnp.where(mask[None, None], s, NEG_INF)
        p = jax.nn.softmax(s, axis=-1).astype(k_lat.dtype)
        return jnp.einsum("bhqk,bkl->bqhl", p, k_lat)

    qlb = jnp.moveaxis(q_lat.reshape(b, nb, blk, h, l), 1, 0)
    qpb = jnp.moveaxis(q_pe.reshape(b, nb, blk, h, r), 1, 0)
    out = lax.map(one_block, (qlb, qpb, q_pos.reshape(nb, blk)))
    return jnp.moveaxis(out, 0, 1).reshape(b, q, h, l)


def mla(x, pos, past, w_in, q_g, kv_g, w_uq, w_ukv, w_o):
    b, s, _ = x.shape
    down = x @ w_in
    c_q = rms_norm(down[..., :Q_LORA], q_g)
    c_kv = rms_norm(down[..., Q_LORA:Q_LORA + KV_LORA], kv_g)
    k_pe = down[..., Q_LORA + KV_LORA:]
    q = jnp.einsum("bsc,chd->bshd", c_q, w_uq)
    q_nope, q_pe = q[..., :QK_NOPE], q[..., QK_NOPE:]
    cos, sin = rope_angles(pos)
    q_pe = apply_rope(q_pe, cos[:, None, :], sin[:, None, :])
    k_pe = apply_rope(k_pe, cos, sin)
    w_uk, w_uv = w_ukv[..., :QK_NOPE], w_ukv[..., QK_NOPE:]
    q_lat = jnp.einsum("bshn,lhn->bshl", q_nope, w_uk)
    if past is None:
        keys_lat, keys_pe = c_kv, k_pe
    else:
        keys_lat = jnp.concatenate([past[0], c_kv], axis=1)
        keys_pe = jnp.concatenate([past[1], k_pe], axis=1)
    o_lat = latent_attention(q_lat, q_pe, keys_lat, keys_pe, pos)
    o = jnp.einsum("bshl,lhv->bshv", o_lat, w_uv).reshape(b, s, MLA_HEADS * V_HEAD)
    return o @ w_o, c_kv, k_pe


def memory_kv(mem, g, w_kv):
    b, m, _ = mem.shape
    k, v = jnp.split(rms_norm(mem, g) @ w_kv, 2, axis=-1)
    return (k.reshape(b, m, XA_HEADS, XA_HEAD_DIM), v.reshape(b, m, XA_HEADS, XA_HEAD_DIM))


def memory_attend(x, k, v, w_q, w_o):
    b, s, _ = x.shape
    q = (x @ w_q).reshape(b, s, XA_HEADS, XA_HEAD_DIM)
    sc = jnp.einsum("bshd,bmhd->bhsm", q, k).astype(jnp.float32) * (XA_HEAD_DIM ** -0.5)
    p = jax.nn.softmax(sc, axis=-1).astype(v.dtype)
    o = jnp.einsum("bhsm,bmhd->bshd", p, v).reshape(b, s, XA_WIDTH)
    return o @ w_o


def trunk(x, pos, conv_prevs, mla_pasts, mem_k, mem_v, P):
    h = x
    conv_new, lat_new, rope_new = [], [], []
    for i in range(DEPTH):
        g = P["norm_gain"][i]
        f = swiglu_ffn(rms_norm(h, g[0]), P["ffn_w_in"][i, 0], P["ffn_w_out"][i, 0])
        h = h + FFN_RESID * rms_norm(f, g[1])
        t = rms_norm(h, g[2])
        j = i // N_MIXERS
        if i % N_MIXERS == 0:
            m, st = conv_module(t, conv_prevs[j], P["conv_w_pw1"][j], P["conv_b_pw1"][j],
                                P["conv_w_dw"][j], P["conv_b_dw"][j], P["conv_ln_g"][j],
                                P["conv_ln_b"][j], P["conv_w_pw2"][j], P["conv_b_pw2"][j])
            conv_new.append(st)
        else:
            m, lat, kr = mla(t, pos, mla_pasts[j], P["mla_w_in"][j], P["mla_q_norm"][j],
                             P["mla_kv_norm"][j], P["mla_w_uq"][j], P["mla_w_ukv"][j], P["mla_w_o"][j])
            lat_new.append(lat)
            rope_new.append(kr)
        h = h + rms_norm(m, g[3])
        c = memory_attend(rms_norm(h, g[4]), mem_k[i], mem_v[i], P["xa_w_q"][i], P["xa_w_o"][i])
        h = h + rms_norm(c, g[5])
        f = swiglu_ffn(rms_norm(h, g[6]), P["ffn_w_in"][i, 1], P["ffn_w_out"][i, 1])
        h = h + FFN_RESID * rms_norm(f, g[7])
    return h, conv_new, lat_new, rope_new


def setup_inputs(seed: int = 0) -> dict:
    key = jax.random.key(seed)
    ks = iter(jax.random.split(key, 48))
    f32 = jnp.float32
    D = D_MODEL

    def nrm(shape, scale=1.0):
        return jax.random.normal(next(ks), shape, f32) * scale

    n_pages = PAST_LEN // PAGE_SIZE
    n_used = DEC_BATCH * n_pages
    n_pool = (5 * n_used + 3) // 4
    page_table = jax.random.permutation(next(ks), n_pool)[:n_used].reshape(
        DEC_BATCH, n_pages).astype(jnp.int32)
    return {
        "x_prompt": nrm((BATCH, SEQ, D)),
        "x_sample": nrm((DEC_BATCH, DEC_SEQ, D)),
        "state_conv_l0": nrm((DEC_BATCH, CONV_W - 1, D), 0.5),
        "state_conv_l2": nrm((DEC_BATCH, CONV_W - 1, D), 0.5),
        "cache_mla_latent_l1": nrm((n_pool, PAGE_SIZE, KV_LORA)),
        "cache_mla_krope_l1": nrm((n_pool, PAGE_SIZE, QK_ROPE)),
        "cache_mla_latent_l3": nrm((n_pool, PAGE_SIZE, KV_LORA)),
        "cache_mla_krope_l3": nrm((n_pool, PAGE_SIZE, QK_ROPE)),
        "cache_mem_k": nrm((DEPTH, DEC_BATCH, N_MEM, XA_HEADS, XA_HEAD_DIM)),
        "cache_mem_v": nrm((DEPTH, DEC_BATCH, N_MEM, XA_HEADS, XA_HEAD_DIM)),
        "page_table": page_table,
        "mem_prompt": nrm((BATCH, N_MEM, D)),
        "norm_gain": 1.0 + nrm((DEPTH, 8, D), 0.05),
        "ffn_w_in": nrm((DEPTH, 2, D, 2 * D_FF), D ** -0.5),
        "ffn_w_out": nrm((DEPTH, 2, D_FF, D), D_FF ** -0.5),
        "conv_w_pw1": nrm((N_CONV_LAYERS, D, 2 * D), D ** -0.5),
        "conv_b_pw1": nrm((N_CONV_LAYERS, 2 * D), 0.02),
        "conv_w_dw": nrm((N_CONV_LAYERS, CONV_W, D), CONV_W ** -0.5),
        "conv_b_dw": nrm((N_CONV_LAYERS, D), 0.02),
        "conv_ln_g": 1.0 + nrm((N_CONV_LAYERS, D), 0.05),
        "conv_ln_b": nrm((N_CONV_LAYERS, D), 0.02),
        "conv_w_pw2": nrm((N_CONV_LAYERS, D, D), D ** -0.5),
        "conv_b_pw2": nrm((N_CONV_LAYERS, D), 0.02),
        "mla_w_in": nrm((N_MLA_LAYERS, D, Q_LORA + KV_LORA + QK_ROPE), D ** -0.5),
        "mla_q_norm": 1.0 + nrm((N_MLA_LAYERS, Q_LORA), 0.05),
        "mla_kv_norm": 1.0 + nrm((N_MLA_LAYERS, KV_LORA), 0.05),
        "mla_w_uq": nrm((N_MLA_LAYERS, Q_LORA, MLA_HEADS, QK_NOPE + QK_ROPE), Q_LORA ** -0.5),
        "mla_w_ukv": nrm((N_MLA_LAYERS, KV_LORA, MLA_HEADS, QK_NOPE + V_HEAD), KV_LORA ** -0.5),
        "mla_w_o": nrm((N_MLA_LAYERS, MLA_HEADS * V_HEAD, D), (MLA_HEADS * V_HEAD) ** -0.5),
        "xa_mem_norm": 1.0 + nrm((DEPTH, D), 0.05),
        "xa_w_q": nrm((DEPTH, D, XA_WIDTH), D ** -0.5),
        "xa_w_kv": nrm((DEPTH, D, 2 * XA_WIDTH), D ** -0.5),
        "xa_w_o": nrm((DEPTH, XA_WIDTH, D), XA_WIDTH ** -0.5),
    }


def reference(x_prompt, x_sample, state_conv_l0, state_conv_l2,
              cache_mla_latent_l1, cache_mla_krope_l1, cache_mla_latent_l3, cache_mla_krope_l3,
              cache_mem_k, cache_mem_v, page_table, mem_prompt,
              norm_gain, ffn_w_in, ffn_w_out,
              conv_w_pw1, conv_b_pw1, conv_w_dw, conv_b_dw, conv_ln_g, conv_ln_b, conv_w_pw2, conv_b_pw2,
              mla_w_in, mla_q_norm, mla_kv_norm, mla_w_uq, mla_w_ukv, mla_w_o,
              xa_mem_norm, xa_w_q, xa_w_kv, xa_w_o):
    P = dict(norm_gain=norm_gain, ffn_w_in=ffn_w_in, ffn_w_out=ffn_w_out,
             conv_w_pw1=conv_w_pw1, conv_b_pw1=conv_b_pw1, conv_w_dw=conv_w_dw, conv_b_dw=conv_b_dw,
             conv_ln_g=conv_ln_g, conv_ln_b=conv_ln_b, conv_w_pw2=conv_w_pw2, conv_b_pw2=conv_b_pw2,
             mla_w_in=mla_w_in, mla_q_norm=mla_q_norm, mla_kv_norm=mla_kv_norm,
             mla_w_uq=mla_w_uq, mla_w_ukv=mla_w_ukv, mla_w_o=mla_w_o,
             xa_w_q=xa_w_q, xa_w_o=xa_w_o)

    b_p, s_p, d = x_prompt.shape
    pos_p = jnp.arange(s_p, dtype=jnp.int32)
    mem_kv_p = [memory_kv(mem_prompt, xa_mem_norm[i], xa_w_kv[i]) for i in range(DEPTH)]
    mem_k_p = [kv[0] for kv in mem_kv_p]
    mem_v_p = [kv[1] for kv in mem_kv_p]
    conv_prev_p = [jnp.zeros((b_p, CONV_W - 1, d), x_prompt.dtype) for _ in range(N_CONV_LAYERS)]
    mla_past_p = [None for _ in range(N_MLA_LAYERS)]
    y_prompt, p_conv, p_lat, p_rope = trunk(x_prompt, pos_p, conv_prev_p, mla_past_p,
                                            mem_k_p, mem_v_p, P)
    mem_k_prompt = jnp.stack(mem_k_p, axis=0)
    mem_v_prompt = jnp.stack(mem_v_p, axis=0)

    n_seq, n_pages = page_table.shape

    def gather(pool):
        return pool[page_table].reshape(n_seq, n_pages * pool.shape[1], pool.shape[2])

    past_len = n_pages * cache_mla_latent_l1.shape[1]
    pos_s = past_len + jnp.arange(x_sample.shape[1], dtype=jnp.int32)
    mla_past_s = [(gather(cache_mla_latent_l1), gather(cache_mla_krope_l1)),
                  (gather(cache_mla_latent_l3), gather(cache_mla_krope_l3))]
    conv_prev_s = [state_conv_l0, state_conv_l2]
    mem_k_s = [cache_mem_k[i] for i in range(DEPTH)]
    mem_v_s = [cache_mem_v[i] for i in range(DEPTH)]
    y_sample, s_conv, s_lat, s_rope = trunk(x_sample, pos_s, conv_prev_s, mla_past_s,
                                            mem_k_s, mem_v_s, P)

    return (y_prompt, y_sample,
            p_conv[0], p_conv[1], p_lat[0], p_rope[0], p_lat[1], p_rope[1],
            mem_k_prompt, mem_v_prompt,
            s_conv[0], s_conv[1], s_lat[0], s_rope[0], s_lat[1], s_rope[1])
```

```python
from concourse.bass_utils import run_bass_kernel_spmd
import numpy as np
import concourse.bass as bass
import concourse.mybir as mybir

F32 = mybir.dt.float32
BF16 = mybir.dt.bfloat16
I32 = mybir.dt.int32
AF = mybir.ActivationFunctionType
ALU = mybir.AluOpType
AX = mybir.AxisListType

ENGS = ("pe", "act", "dve", "pool", "sp")
SEM_EPOCH = 20000
N_DMA_SEMS = 24


class Buf:
    __slots__ = ("name", "writers", "readers", "war", "excl")

    def __init__(self, name, excl=False):
        self.name = name
        self.excl = excl
        self.writers = []
        self.readers = []
        self.war = []


class Op:
    __slots__ = ("eng", "fn", "deps", "idx", "sig", "cnt", "dma", "dsem", "dval", "dprev")

    def __init__(self, eng, fn):
        self.eng = eng
        self.fn = fn
        self.deps = set()
        self.sig = False
        self.dma = False


class Prog:
    def __init__(self, nc):
        self.nc = nc
        self.ops = []
        self.streams = {e: [] for e in ENGS}
        self.bar = {e: None for e in ENGS}

    def op(self, eng, fn, reads=(), writes=(), pwrites=(), dma=False):
        o = Op(eng, fn)
        o.dma = dma
        o.idx = len(self.ops)
        for b in reads:
            o.deps.update(b.writers)
            if b.excl:
                o.deps.update(b.readers)
        for b in writes:
            o.deps.update(b.readers)
            o.deps.update(b.writers)
            o.deps.update(b.war)
        for b in pwrites:
            o.deps.update(b.readers)
            o.deps.update(b.war)
        for b in reads:
            b.readers.append(o)
        for b in writes:
            b.writers = [o]
            b.readers = []
            b.war = []
        for b in pwrites:
            if b.readers:
                b.war = b.readers
                b.readers = []
                b.writers = [o]
            else:
                b.writers.append(o)
        if self.bar[eng] is not None:
            o.deps.update(self.bar[eng])
            self.bar[eng] = None
        o.deps.discard(o)
        self.ops.append(o)
        self.streams[eng].append(o)
        return o

    def barrier(self):
        last = [st[-1] for st in self.streams.values() if st]
        for e in ENGS:
            prev = self.bar[e] or []
            self.bar[e] = list(prev) + last

    def emit(self):
        nc = self.nc
        ops = self.ops
        for o in ops:
            for d in o.deps:
                if d.eng == "pe" and o.eng == "pe":
                    continue
                d.sig = True
        cnt = {e: 0 for e in ENGS}
        ndma = {"sp": 0, "pool": 0, "act": 0}
        dma_hist = {}
        for o in ops:
            if o.dma:
                o.sig = True
        dma_last = {}
        for o in ops:
            if o.dma:
                k = ndma[o.eng]
                ndma[o.eng] += 1
                slot = k % N_DMA_SEMS
                key = (o.eng, slot)
                prev = dma_last.get(key, 0)
                o.dsem = key
                o.dprev = prev
                o.dval = prev + 1
                dma_last[key] = prev + 1
            elif o.sig:
                cnt[o.eng] += 1
                o.cnt = cnt[o.eng]
        self._sem_cms = []
        def newsem(name):
            cm = nc.semaphore(name)
            s = cm.__enter__()
            self._sem_cms.append(cm)
            return s
        csem = {}
        for e in ("pe", "act", "dve", "pool"):
            n = cnt[e] // SEM_EPOCH + 1
            csem[e] = [newsem(f"c_{e}_{i}") for i in range(n)]
        dsem = {}
        for (e, slot) in dma_last:
            dsem[(e, slot)] = newsem(f"d_{e}_{slot}")
        self.csem, self.dsem = csem, dsem

        def target(d):
            if d.dma:
                return dsem[d.dsem], 16 * d.dval
            ep, v = divmod(d.cnt - 1, SEM_EPOCH)
            return csem[d.eng][ep], v + 1

        engobj = {"pe": nc.tensor, "act": nc.scalar, "dve": nc.vector, "pool": nc.gpsimd, "sp": nc.sync}

        def emit_stream(e, eng):
            seen = {}
            for o in self.streams[e]:
                need = {}
                for d in o.deps:
                    if d.eng == "pe" and e == "pe" and not d.dma:
                        continue
                    s, v = target(d)
                    k = id(s)
                    if seen.get(k, 0) >= v:
                        continue
                    if k not in need or need[k][1] < v:
                        need[k] = (s, v)
                if o.dma and o.dprev > 0:
                    s = dsem[o.dsem]
                    v = 16 * o.dprev
                    k = id(s)
                    if seen.get(k, 0) < v and (k not in need or need[k][1] < v):
                        need[k] = (s, v)
                for k, (s, v) in need.items():
                    eng.wait_ge(s, v)
                    seen[k] = v
                if o.fn is None:
                    continue
                ins = o.fn(eng)
                if o.dma:
                    ins.then_inc(dsem[o.dsem], 16)
                elif o.sig:
                    ep = (o.cnt - 1) // SEM_EPOCH
                    ins.then_inc(csem[e][ep], 1)

        with nc.Block() as block:
            @block.tensor
            def _(eng):
                emit_stream("pe", eng)

            @block.scalar
            def _(eng):
                emit_stream("act", eng)

            @block.vector
            def _(eng):
                emit_stream("dve", eng)

            @block.gpsimd
            def _(eng):
                emit_stream("pool", eng)

            @block.sync
            def _(eng):
                emit_stream("sp", eng)
        for cm in reversed(self._sem_cms):
            cm.__exit__(None, None, None)

from contextlib import ExitStack

D = 1024
SEQ = 2048
NS = 128
NT = SEQ + NS
DFF = 2816
NFF = DFF // 128
GROUPS = [(0, 512), (512, 512), (1024, 512), (1536, 512), (2048, 128)]
RMS_EPS = 1e-6
LN_EPS = 1e-5
ARENA_F32 = 19200

R_GAIN = 0
R_BPW1 = 32
R_BDW = 36
R_LNG = 38
R_LNB = 40
R_BPW2 = 42
R_MEMN = 44
R_WDW = 48
R_QN = 110
R_KVN = 112
NROWS = 114


def build(nsub=16, n_pool=10240):
    nc = bass.Bass("TRN2", target_bir_lowering=False)
    P = Prog(nc)
    es = ExitStack()

    def din(name, shape, dt=F32):
        return nc.dram_tensor(name, list(shape), dt, kind="ExternalInput").ap()

    def dout(name, shape, dt=F32):
        return nc.dram_tensor(name, list(shape), dt, kind="ExternalOutput").ap()

    def sb(name, shape, dt=F32):
        return es.enter_context(nc.sbuf_tensor(name, list(shape), dt))

    xp = din("xp", [SEQ, D]); xs = din("xs", [NS, D])
    vtab = din("vtab", [128, D])
    ffn_w_in = din("ffn_w_in", [4, 2, D, 2 * DFF]); ffn_w_out = din("ffn_w_out", [4, 2, DFF, D])
    consts = din("consts", [128, 512])
    rope = din("rope", [2, 32, NT])
    iota = din("iota", [128, 2])
    cs_in = [din("cs0", [16, 30, D]), din("cs2", [16, 30, D])]
    lat_pool = [din("lat1", [n_pool * 128, 256]), din("lat3", [n_pool * 128, 256])]
    kr_pool = [din("kr1", [n_pool * 128, 32]), din("kr3", [n_pool * 128, 32])]
    memk_in = din("memk", [4, 16, 256, 512]); memv_in = din("memv", [4, 16, 256, 512])
    pt_in = din("pt", [16, 64], I32)
    memp = din("memp", [256, D])
    conv_w_pw1 = din("conv_w_pw1", [2, D, 2 * D]); conv_w_pw2 = din("conv_w_pw2", [2, D, D])
    mla_w_in = din("mla_w_in", [2, D, 672]); mla_w_uq = din("mla_w_uq", [2, 384, 1536])
    mla_w_ukv = din("mla_w_ukv", [2, 256, 2048]); mla_w_o = din("mla_w_o", [2, D, D])
    xa_w_q = din("xa_w_q", [4, D, 512]); xa_w_kv = din("xa_w_kv", [4, D, D]); xa_w_o = din("xa_w_o", [4, 512, D])
    y_p = dout("y_p", [SEQ, D]); y_s = dout("y_s", [NS, D])
    conv_p = [dout("conv0_p", [30, D]), dout("conv2_p", [30, D])]
    lat_p = [dout("lat1_p", [SEQ, 256]), dout("lat3_p", [SEQ, 256])]
    kr_p = [dout("kr1_p", [SEQ, 32]), dout("kr3_p", [SEQ, 32])]
    memk_p = dout("memk_p", [4, 256, 512]); memv_p = dout("memv_p", [4, 256, 512])
    conv_s = [dout("conv0_s", [16, 30, D]), dout("conv2_s", [16, 30, D])]
    lat_s = [dout("lat1_s", [NS, 256]), dout("lat3_s", [NS, 256])]
    kr_s = [dout("kr1_s", [NS, 32]), dout("kr3_s", [NS, 32])]

    h = sb("h", [128, 8, NT])
    vecs = sb("vecs", [128, 8, 128])
    identf = sb("identf", [128, 128]); identb = sb("identb", [128, 128], BF16); onesb = sb("onesb", [128, 128], BF16)
    wst = [sb(f"wst{i}", [128, 2816]) for i in range(2)]
    wbf = [sb(f"wbf{i}", [128, 2816], BF16) for i in range(2)]
    xn = sb("xn", [128, 8, 512], BF16)
    rstd = [sb(f"rstd{i}", [128, 512]) for i in range(2)]
    tmpf = [sb(f"tmpf{i}", [128, 512]) for i in range(2)]
    sqb = [sb(f"sqb{i}", [128, 512], BF16) for i in range(2)]
    tin = [sb(f"tin{i}", [128, D]) for i in range(2)]
    arena = sb("arena", [128, ARENA_F32])
    ps = es.enter_context(nc.psum_tensor("ps", [128, 8, 512], F32))

    B = lambda n: Buf(n)
    b_h = [[B(f"h{c}_{t}") for t in range(NT // 128)] for c in range(8)]
    hb = lambda c, t0, n: b_h[c][t0 // 128:(t0 + n + 127) // 128]
    b_vecs = B("vecs"); b_const = B("const")
    b_wst = [B("wst0"), B("wst1")]; b_wbf = [B("wbf0"), B("wbf1")]
    b_xn = B("xn"); b_rstd = [B("rstd0"), B("rstd1")]; b_tmpf = [B("tmpf0"), B("tmpf1")]
    b_sqb = [B("sqb0"), B("sqb1")]; b_tin = [B("tin0"), B("tin1")]
    b_ps = [Buf(f"ps{i}", excl=True) for i in range(8)]
    st = {"bank": 0, "banks": list(range(8)), "w": 0, "rstd": 0, "tmpf": 0, "sqb": 0, "tin": 0, "ar": 0}

    bq = list(range(8))

    def bank():
        i = bq.pop(0)
        bq.append(i)
        return i

    def bank_hold():
        return bq.pop(0)

    def bank_release(i):
        bq.append(i)

    def rot(key, n=2):
        i = st[key] % n
        st[key] += 1
        return i

    def gidx(t0):
        return [g for g, (a, n) in enumerate(GROUPS) if a == t0][0]

    def ar_reset():
        P.barrier()
        st["ar"] = 0

    def ar(shape, dt=F32):
        n = int(np.prod(shape[1:]))
        words = (n + 1) // 2 if dt == BF16 else n
        a = st["ar"]
        st["ar"] += words
        assert st["ar"] <= ARENA_F32, ("arena overflow", st["ar"])
        v = arena[:, a:a + words]
        if dt != F32:
            v = v.bitcast(dt)[:, 0:n]
        if len(shape) == 3:
            v = v.rearrange("p (a b) -> p a b", a=shape[1])
        return v

    P.op("sp", lambda e: e.dma_start(out=identf[:], in_=consts[:, 0:128]), writes=[b_const], dma=True)
    P.op("dve", lambda e: e.tensor_copy(out=identb[:], in_=identf[:]), reads=[b_const], pwrites=[b_const])
    P.op("dve", lambda e: e.memset(onesb[:], 1.0), pwrites=[b_const])
    maskf = sb("maskf", [128, 136]); maskb = sb("maskb", [128, 136], BF16); iot = sb("iot", [128, 2])
    P.op("sp", lambda e: e.dma_start(out=maskf[:], in_=consts[:, 128:264]), pwrites=[b_const], dma=True)
    P.op("sp", lambda e: e.dma_start(out=iot[:], in_=iota[:, :]), pwrites=[b_const], dma=True)
    P.op("dve", lambda e: e.tensor_copy(out=maskb[:], in_=maskf[:]), reads=[b_const], pwrites=[b_const])

    def mm(out_ap, pairs, reads, bk):
        def fn(e, pairs=pairs, out_ap=out_ap):
            ins = None
            for i, (l, r) in enumerate(pairs):
                ins = e.matmul(out_ap, lhsT=l, rhs=r, start=(i == 0), stop=(i == len(pairs) - 1))
            return ins
        return P.op("pe", fn, reads=reads, writes=[b_ps[bk]])

    i = rot("tin")
    P.op("sp", lambda e: e.dma_start(out=tin[i][:], in_=vtab[:, :]), writes=[b_tin[i]], dma=True)
    for half in range(2):
        bk = bank()
        for cc in range(4):
            c = half * 4 + cc
            P.op("pe", lambda e, c=c, cc=cc, bk=bk: e.matmul(ps[:, bk, cc * 128:(cc + 1) * 128], lhsT=tin[i][:, c * 128:(c + 1) * 128],
                                                           rhs=identf[:], start=True, stop=True),
                 reads=[b_tin[i], b_const], pwrites=[b_ps[bk]])
        P.op("act", lambda e, half=half, bk=bk: e.copy(out=vecs[:, half * 4:half * 4 + 4, :],
                                                       in_=ps[:, bk, :].rearrange("p (a b) -> p a b", a=4)),
             reads=[b_ps[bk]], pwrites=[b_vecs])

    def vec(r, c, n=128):
        return vecs[0:n, c, r:r + 1]

    def load_tokens(src, ntok, tok0):
        for tt in range(ntok // 128):
            i = rot("tin")
            P.op("sp", lambda e, i=i, tt=tt: e.dma_start(out=tin[i][:], in_=src[tt * 128:(tt + 1) * 128, :]),
                 writes=[b_tin[i]], dma=True)
            t = tok0 + tt * 128
            for half in range(2):
                bk = bank()
                for cc in range(4):
                    c = half * 4 + cc
                    P.op("pe", lambda e, i=i, c=c, cc=cc, bk=bk: e.matmul(ps[:, bk, cc * 128:(cc + 1) * 128],
                                                                         lhsT=tin[i][:, c * 128:(c + 1) * 128], rhs=identf[:],
                                                                         start=True, stop=True),
                         reads=[b_tin[i], b_const], pwrites=[b_ps[bk]])
                P.op("act", lambda e, half=half, bk=bk, t=t: e.copy(out=h[:, half * 4:half * 4 + 4, t:t + 128],
                                                                   in_=ps[:, bk, :].rearrange("p (a b) -> p a b", a=4)),
                     reads=[b_ps[bk]], pwrites=[b_h[c][t // 128] for c in range(half * 4, half * 4 + 4)])

    out_dmas = []

    def store_rows(dst, srcs, ntok):
        for tt in range(ntok // 128):
            i = rot("tin")
            col = 0
            bk = None
            used = 0
            evs = []
            for (apf, msz, bufs) in srcs:
                if bk is None or used + msz > 512:
                    if bk is not None:
                        evs.append((bk, col - used, used))
                    bk = bank(); used = 0
                P.op("pe", lambda e, apf=apf, msz=msz, bk=bk, used=used, tt=tt: e.matmul(
                    ps[:, bk, used:used + msz], lhsT=apf(tt * 128), rhs=identf[0:msz, 0:msz], start=True, stop=True),
                    reads=list(bufs) + [b_const], pwrites=[b_ps[bk]])
                used += msz; col += msz
            evs.append((bk, col - used, used))
            for (bk, c0, n) in evs:
                P.op("act", lambda e, i=i, bk=bk, c0=c0, n=n: e.copy(out=tin[i][:, c0:c0 + n], in_=ps[:, bk, 0:n]),
                     reads=[b_ps[bk]], pwrites=[b_tin[i]])
            o = P.op("sp", lambda e, i=i, tt=tt, col=col: e.dma_start(out=dst[tt * 128:(tt + 1) * 128, 0:col], in_=tin[i][:, 0:col]),
                     reads=[b_tin[i]], dma=True)
            out_dmas.append(o)

    def stats_bc(srcs, n, square=True):
        bk = bank()
        for j, (ap, ksz, bufs) in enumerate(srcs):
            i = rot("sqb")
            if square:
                P.op("act", lambda e, i=i, ap=ap, ksz=ksz: e.activation(out=sqb[i][0:ksz, 0:n], in_=ap, func=AF.Square),
                     reads=bufs, writes=[b_sqb[i]])
            else:
                P.op("act", lambda e, i=i, ap=ap, ksz=ksz: e.copy(out=sqb[i][0:ksz, 0:n], in_=ap),
                     reads=bufs, writes=[b_sqb[i]])
            P.op("pe", lambda e, i=i, ksz=ksz, j=j, bk=bk, last=(j == len(srcs) - 1): e.matmul(
                ps[:, bk, 0:n], lhsT=onesb[0:ksz, :], rhs=sqb[i][0:ksz, 0:n], start=(j == 0), stop=last),
                reads=[b_sqb[i], b_const], pwrites=[b_ps[bk]])
        return bk

    def rstd_from(bk, n, dim, eps):
        i = rot("rstd")
        P.op("act", lambda e, i=i: e.activation(out=rstd[i][:, 0:n], in_=ps[:, bk, 0:n], func=AF.Sqrt, scale=1.0 / dim, bias=eps),
             reads=[b_ps[bk]], writes=[b_rstd[i]])
        P.op("dve", lambda e, i=i: e.reciprocal(out=rstd[i][:, 0:n], in_=rstd[i][:, 0:n]), reads=[b_rstd[i]], writes=[b_rstd[i]])
        return i

    def subtiles(n):
        return [(o, min(512, n - o)) for o in range(0, n, 512)]

    def prenorm(t0, n, grow, dst=None, b_dst=None):
        dst = xn if dst is None else dst
        b_dst = b_xn if b_dst is None else b_dst
        for (o, m) in subtiles(n):
            bk = stats_bc([(h[:, c, t0 + o:t0 + o + m], 128, hb(c, t0 + o, m)) for c in range(8)], m)
            ri = rstd_from(bk, m, D, RMS_EPS)
            for c in range(8):
                P.op("dve", lambda e, c=c, o=o, m=m, ri=ri: e.scalar_tensor_tensor(out=dst[:, c, o:o + m], in0=h[:, c, t0 + o:t0 + o + m],
                                                                                  scalar=vec(grow, c), in1=rstd[ri][:, 0:m],
                                                                                  op0=ALU.mult, op1=ALU.mult),
                     reads=hb(c, t0 + o, m) + [b_vecs, b_rstd[ri]], pwrites=[b_dst])

    def postnorm(t0, n, grow, a, fo, b_fo):
        for (o, m) in subtiles(n):
            bk = stats_bc([(fo[:, c, o:o + m], 128, [b_fo]) for c in range(8)], m)
            ri = rstd_from(bk, m, D, RMS_EPS)
            for c in range(8):
                i = rot("tmpf")
                P.op("dve", lambda e, c=c, i=i, o=o, m=m, ri=ri: e.scalar_tensor_tensor(out=tmpf[i][:, 0:m], in0=fo[:, c, o:o + m], scalar=vec(grow, c),
                                                                                       in1=rstd[ri][:, 0:m], op0=ALU.mult, op1=ALU.mult),
                     reads=[b_fo, b_vecs, b_rstd[ri]], writes=[b_tmpf[i]])
                P.op("dve", lambda e, c=c, i=i, o=o, m=m: e.scalar_tensor_tensor(out=h[:, c, t0 + o:t0 + o + m], in0=tmpf[i][:, 0:m], scalar=float(a),
                                                                                in1=h[:, c, t0 + o:t0 + o + m], op0=ALU.mult, op1=ALU.add),
                     reads=[b_tmpf[i]] + hb(c, t0 + o, m), writes=hb(c, t0 + o, m))

    def cast(out_ap, in_ap, reads, wkw):
        st["cast"] = st.get("cast", 0) + 1
        if st["cast"] % 2:
            P.op("act", lambda e: e.copy(out=out_ap, in_=in_ap), reads=reads, **wkw)
        else:
            P.op("dve", lambda e: e.tensor_copy(out=out_ap, in_=in_ap), reads=reads, **wkw)

    def load_w(segs, K):
        nk = (K + 127) // 128
        kp = min(K, 128)
        tot = sum(s[2] for s in segs)
        assert nk * tot <= 2816
        i = rot("w")
        sv = wst[i][:, 0:nk * tot].rearrange("p (k m) -> p k m", k=nk)
        bv = wbf[i][:, 0:nk * tot].rearrange("p (k m) -> p k m", k=nk)
        off = 0
        for (w2, c0, ncol) in segs:
            src = w2[:, c0:c0 + ncol].rearrange("(k p) m -> p k m", p=kp)
            P.op("sp", lambda e, src=src, off=off, ncol=ncol: e.dma_start(out=sv[0:kp, :, off:off + ncol], in_=src),
                 pwrites=[b_wst[i]], dma=True)
            off += ncol
        cast(bv[0:kp], sv[0:kp], [b_wst[i]], dict(writes=[b_wbf[i]]))
        return bv, b_wbf[i], nk

    FGROUPS = [(0, 768), (768, 768), (1536, 640)]

    def ffn(l, j, gpre, gpost):
        ar_reset()
        fxn = ar([128, 8, 768], BF16); b_fxn = B("fxn")
        hid = ar([128, NFF, 768], BF16); b_hid = B("hid")
        fo = ar([128, 8, 768]); b_fo = B("fo")
        w_in = ffn_w_in[l, j]; w_out = ffn_w_out[l, j]
        def grp(t0, n):
            prenorm(t0, n, R_GAIN + l * 8 + gpre, fxn, b_fxn)
            for f in range(NFF):
                wb, bw, nk = load_w([(w_in, f * 128, 128), (w_in, DFF + f * 128, 128)], D)
                for (o, m) in subtiles(n):
                    bg = bank(); bu = bank()
                    mm(ps[:, bg, 0:m], [(wb[:, k, 0:128], fxn[:, k, o:o + m]) for k in range(8)], [bw, b_fxn], bg)
                    mm(ps[:, bu, 0:m], [(wb[:, k, 128:256], fxn[:, k, o:o + m]) for k in range(8)], [bw, b_fxn], bu)
                    i = rot("tmpf")
                    P.op("act", lambda e, i=i, bg=bg, m=m: e.activation(out=tmpf[i][:, 0:m], in_=ps[:, bg, 0:m], func=AF.Silu),
                         reads=[b_ps[bg]], writes=[b_tmpf[i]])
                    P.op("dve", lambda e, i=i, bu=bu, f=f, o=o, m=m: e.tensor_tensor(out=hid[:, f, o:o + m], in0=tmpf[i][:, 0:m], in1=ps[:, bu, 0:m], op=ALU.mult),
                         reads=[b_tmpf[i], b_ps[bu]], pwrites=[b_hid])
            for c in range(8):
                wb, bw, nk = load_w([(w_out, c * 128, 128)], DFF)
                for (o, m) in subtiles(n):
                    bo = bank()
                    mm(ps[:, bo, 0:m], [(wb[:, f, 0:128], hid[:, f, o:o + m]) for f in range(NFF)], [bw, b_hid], bo)
                    P.op("act", lambda e, c=c, bo=bo, o=o, m=m: e.copy(out=fo[:, c, o:o + m], in_=ps[:, bo, 0:m]), reads=[b_ps[bo]], pwrites=[b_fo])
            postnorm(t0, n, R_GAIN + l * 8 + gpost, 0.5, fo, b_fo)
        for (t0, n) in FGROUPS:
            grp(t0, n)

    def mmx(items, reads, bk):
        def fn(e, items=items):
            ins = None
            for i, (o_, l_, r_) in enumerate(items):
                ins = e.matmul(o_, lhsT=l_, rhs=r_, start=(i == 0), stop=(i == len(items) - 1))
            return ins
        return P.op("pe", fn, reads=reads, writes=[b_ps[bk]])

    def tr_block(out_ap, in_ap, kk, bk, reads, f32=False):
        idt = identf if f32 else identb
        return P.op("pe", lambda e: e.matmul(out_ap, lhsT=in_ap, rhs=idt[0:kk, 0:kk], start=True, stop=True),
                    reads=list(reads) + [b_const], pwrites=[b_ps[bk]])

    def load_w_to(segs, K, dst, b_dst):
        nk = (K + 127) // 128
        kp = min(K, 128)
        tot = sum(s_[2] for s_ in segs)
        assert nk * tot <= 2816
        i = rot("w")
        sv = wst[i][:, 0:nk * tot].rearrange("p (k m) -> p k m", k=nk)
        off = 0
        for (w2, c0, ncol) in segs:
            src = w2[:, c0:c0 + ncol].rearrange("(k p) m -> p k m", p=kp)
            P.op("sp", lambda e, src=src, off=off, ncol=ncol: e.dma_start(out=sv[0:kp, :, off:off + ncol], in_=src),
                 pwrites=[b_wst[i]], dma=True)
            off += ncol
        cast(dst, sv[0:kp], [b_wst[i]], dict(pwrites=[b_dst]))

    def out_linear(w2d, K, rhs_fn, rhs_bufs, n, fo, b_fo, bias_row=None):
        nk = (K + 127) // 128
        for c in range(8):
            wb, bw, _ = load_w([(w2d, c * 128, 128)], K)
            bo = bank()
            mm(ps[:, bo, 0:n], [(wb[:, k, 0:128], rhs_fn(k)) for k in range(nk)], [bw] + list(rhs_bufs), bo)
            if bias_row is None:
                P.op("act", lambda e, c=c, bo=bo: e.copy(out=fo[:, c, 0:n], in_=ps[:, bo, 0:n]), reads=[b_ps[bo]], pwrites=[b_fo])
            else:
                P.op("act", lambda e, c=c, bo=bo: e.activation(out=fo[:, c, 0:n], in_=ps[:, bo, 0:n], func=AF.Identity,
                                                               bias=vec(bias_row, c), scale=1.0),
                     reads=[b_ps[bo], b_vecs], pwrites=[b_fo])

    def flash_scratch(nsets=2):
        scs = []
        for i in range(nsets):
            Pb = ar([128, 512], BF16); bP = B("P")
            scs.append(dict(P=Pb, PT=ar([128, 4, 128], BF16), oacc=ar([128, 256]), on=Pb[:, 0:256],
                            st=ar([128, 16]), bP=bP, bPT=B("PT"), boacc=B("oacc"), bon=bP,
                            bst={k: B("st" + k) for k in ("m0", "m1", "negm", "rs", "l", "alpha", "rinv", "alpha1")}))
        return scs

    def run_gens(gens):
        live = list(gens)
        while live:
            nxt = []
            for g_ in live:
                try:
                    next(g_)
                    nxt.append(g_)
                except StopIteration:
                    pass
            live = nxt

    def flash(*a, **k):
        run_gens([flash_gen(*a, **k)])
    STC = {"m0": 0, "m1": 1, "negm": 2, "rs": 3, "l": 4, "alpha": 5, "rinv": 6, "alpha1": 7}

    def flash_gen(nq, QT, chunks, dv, scale, sc, dst_fn, sc2=None):
        stc = lambda k: sc["st"][0:nq, STC[k]:STC[k] + 1]
        bst = sc["bst"]
        qreads = sum([list(b_) for (_, b_) in QT], [])
        n_ch = len(chunks)
        made = {}
        pb = lambda ci: sc if (sc2 is None or ci % 2 == 0) else sc2
        alpha = lambda ci: "alpha" if ci % 2 == 0 else "alpha1"

        def get(i_):
            if i_ not in made:
                made[i_] = chunks[i_]() if callable(chunks[i_]) else chunks[i_]
            return made[i_]

        def emit_S(ch):
            nk = ch["nk"]
            bk = bank_hold()
            S = ps[0:nq, bk, 0:nk]
            items = [(S, q_, k_) for (q_, _), (k_, _) in zip(QT, ch["KT"])]
            reads = qreads + sum([list(b_) for (_, b_) in ch["KT"]], [])
            if ch.get("mask") is not None:
                c0, map_, w = ch["mask"]
                items.append((ps[0:nq, bk, c0:c0 + w], identb[0:nq, 0:nq], map_))
                reads.append(b_const)
            mmx(items, reads, bk)
            return bk, S
        Sq = {}

        def stats(ci):
            ch = get(ci)
            nk = ch["nk"]
            bk, S = Sq.pop(ci)
            first = ci == 0
            mnew, mold = ("m0", "m1") if ci % 2 == 0 else ("m1", "m0")
            Pb = pb(ci)
            if first:
                P.op("dve", lambda e: e.reduce_max(out=stc(mnew), in_=S, axis=AX.X), reads=[b_ps[bk]], writes=[bst[mnew]])
            else:
                P.op("dve", lambda e: e.reduce_max(out=stc("rinv"), in_=S, axis=AX.X), reads=[b_ps[bk]], writes=[bst["rinv"]])
                P.op("dve", lambda e: e.tensor_tensor(out=stc(mnew), in0=stc(mold), in1=stc("rinv"), op=ALU.max),
                     reads=[bst[mold], bst["rinv"]], writes=[bst[mnew]])
            P.op("dve", lambda e: e.tensor_scalar(out=stc("negm"), in0=stc(mnew), scalar1=-float(scale), scalar2=None, op0=ALU.mult),
                 reads=[bst[mnew]], writes=[bst["negm"]])
            acc = "l" if first else "rs"
            P.op("act", lambda e: e.activation(out=Pb["P"][0:nq, 0:nk], in_=S, func=AF.Exp, bias=stc("negm"),
                                               scale=float(scale), accum_out=stc(acc)),
                 reads=[b_ps[bk], bst["negm"]], writes=[Pb["bP"], bst[acc]])
            if not first:
                al = alpha(ci)
                P.op("act", lambda e: e.activation(out=stc(al), in_=stc(mold), func=AF.Exp, bias=stc("negm"), scale=float(scale)),
                     reads=[bst[mold], bst["negm"]], writes=[bst[al]])
                P.op("dve", lambda e: e.scalar_tensor_tensor(out=stc("l"), in0=stc("l"), scalar=stc(al), in1=stc("rs"),
                                                             op0=ALU.mult, op1=ALU.add),
                     reads=[bst["l"], bst[al], bst["rs"]], writes=[bst["l"]])
            bank_release(bk)

        def tail(ci):
            ch = get(ci)
            nk = ch["nk"]
            first = ci == 0
            Pb = pb(ci)
            bkt = bank()
            blks = [(kb * 128, min(128, nk - kb * 128)) for kb in range((nk + 127) // 128)]
            for kb, (k0, ksz) in enumerate(blks):
                tr_block(ps[0:ksz, bkt, kb * nq:(kb + 1) * nq], Pb["P"][0:nq, k0:k0 + ksz], nq, bkt, [Pb["bP"]])
            nb = len(blks)
            kmax = 128 if nb > 1 else blks[0][1]
            src = ps[0:kmax, bkt, 0:nb * nq].rearrange("p (a b) -> p a b", a=nb)
            if ci % 2:
                P.op("dve", lambda e: e.tensor_copy(out=Pb["PT"][0:kmax, 0:nb, 0:nq], in_=src), reads=[b_ps[bkt]], writes=[Pb["bPT"]])
            else:
                P.op("act", lambda e: e.copy(out=Pb["PT"][0:kmax, 0:nb, 0:nq], in_=src), reads=[b_ps[bkt]], writes=[Pb["bPT"]])
            yield
            bko = bank()
            items = [(ps[0:nq, bko, 0:dv], Pb["PT"][0:ksz, kb, 0:nq], ch["V"][kb][0]) for kb, (k0, ksz) in enumerate(blks)]
            reads = [Pb["bPT"]] + sum([list(v_[2]) for v_ in ch["V"]], [])
            mmx(items, reads, bko)
            if first:
                P.op("act", lambda e: e.copy(out=sc["oacc"][0:nq, 0:dv], in_=ps[0:nq, bko, 0:dv]), reads=[b_ps[bko]], writes=[sc["boacc"]])
            else:
                al = alpha(ci)
                P.op("dve", lambda e: e.scalar_tensor_tensor(out=sc["oacc"][0:nq, 0:dv], in0=sc["oacc"][0:nq, 0:dv], scalar=stc(al),
                                                             in1=ps[0:nq, bko, 0:dv], op0=ALU.mult, op1=ALU.add),
                     reads=[sc["boacc"], bst[al], b_ps[bko]], writes=[sc["boacc"]])

        if sc2 is None:
            Sq[0] = emit_S(get(0))
            for ci in range(n_ch):
                if ci + 1 < n_ch:
                    Sq[ci + 1] = emit_S(get(ci + 1))
                stats(ci)
                yield
                yield from tail(ci)
                yield
        else:
            Sq[0] = emit_S(get(0))
            stats(0)
            if n_ch > 1:
                Sq[1] = emit_S(get(1))
            for ci in range(n_ch):
                if ci + 1 < n_ch:
                    stats(ci + 1)
                yield from tail(ci)
                if ci + 2 < n_ch:
                    Sq[ci + 2] = emit_S(get(ci + 2))
                yield
        P.op("dve", lambda e: e.reciprocal(out=stc("rinv"), in_=stc("l")), reads=[bst["l"]], writes=[bst["rinv"]])
        P.op("dve", lambda e: e.tensor_scalar(out=sc["on"][0:nq, 0:dv], in0=sc["oacc"][0:nq, 0:dv], scalar1=stc("rinv"), scalar2=None, op0=ALU.mult),
             reads=[sc["boacc"], bst["rinv"]], writes=[sc["bon"]])
        yield
        bkf = bank()
        nb = dv // 128
        for blk in range(nb):
            tr_block(ps[:, bkf, blk * nq:(blk + 1) * nq], sc["on"][0:nq, blk * 128:(blk + 1) * 128], nq, bkf, [sc["bon"]])
        r_ = dst_fn(ps[:, bkf, 0:nb * nq].rearrange("p (a b) -> p a b", a=nb), bkf)
        if r_ is not None:
            yield from r_

    def xattn(l):
        ar_reset()
        g_pre, g_post = R_GAIN + l * 8 + 4, R_GAIN + l * 8 + 5
        scale = 128 ** -0.5
        memT = ar([128, 8, 256]); b_memT = B("memT")
        memn = ar([128, 8, 256], BF16); b_memn = B("memn")
        KT = ar([128, 4, 256], BF16); b_KT = B("KT")
        Vn = ar([128, 2, 512], BF16); b_Vn = B("Vn")
        kvout = ar([128, 2, 512]); b_kvout = B("kvout")
        qT = ar([128, 4, 512], BF16); b_qT = B("qT")
        oT = ar([128, 4, 512], BF16); b_oT = B("oT")
        fo = ar([128, 8, 512]); b_fo = B("fo")
        scs = flash_scratch(4)
        sKs = ar([128, 2, 512]); b_sKs = B("sKs")
        sKb = ar([128, 2, 512], BF16); b_sKb = B("sKb")
        sVb = ar([128, 2, 512], BF16); b_sVb = B("sVb")
        sKT = ar([128, 4, 256], BF16); b_sKT = B("sKT")
        for tt in range(2):
            i = rot("tin")
            P.op("sp", lambda e, i=i, tt=tt: e.dma_start(out=tin[i][:], in_=memp[tt * 128:(tt + 1) * 128, :]), writes=[b_tin[i]], dma=True)
            for half in range(2):
                bk = bank()
                for cc in range(4):
                    c = half * 4 + cc
                    tr_block(ps[:, bk, cc * 128:(cc + 1) * 128], tin[i][:, c * 128:(c + 1) * 128], 128, bk, [b_tin[i]], f32=True)
                P.op("act", lambda e, half=half, bk=bk, tt=tt: e.copy(out=memT[:, half * 4:half * 4 + 4, tt * 128:(tt + 1) * 128],
                                                                     in_=ps[:, bk, :].rearrange("p (a b) -> p a b", a=4)),
                     reads=[b_ps[bk]], pwrites=[b_memT])
        bk = stats_bc([(memT[:, c, :], 128, [b_memT]) for c in range(8)], 256)
        ri = rstd_from(bk, 256, D, RMS_EPS)
        for c in range(8):
            P.op("dve", lambda e, c=c: e.scalar_tensor_tensor(out=memn[:, c, :], in0=memT[:, c, :], scalar=vec(R_MEMN + l, c),
                                                               in1=rstd[ri][:, 0:256], op0=ALU.mult, op1=ALU.mult),
                 reads=[b_memT, b_vecs, b_rstd[ri]], pwrites=[b_memn])
        wkv = xa_w_kv[l]
        for hh in range(4):
            wb, bw, _ = load_w([(wkv, hh * 128, 128)], D)
            bk = bank()
            mm(ps[:, bk, 0:256], [(wb[:, k, 0:128], memn[:, k, :]) for k in range(8)], [bw, b_memn], bk)
            P.op("act", lambda e, hh=hh, bk=bk: e.copy(out=KT[:, hh, :], in_=ps[:, bk, 0:256]), reads=[b_ps[bk]], pwrites=[b_KT])
        for which in range(2):
            for half in range(2):
                wb, bw, _ = load_w([(wkv, which * 512 + half * 256, 256)], D)
                for mt in range(2):
                    bk = bank()
                    mm(ps[:, bk, 0:256], [(memn[:, k, mt * 128:(mt + 1) * 128], wb[:, k, 0:256]) for k in range(8)], [bw, b_memn], bk)
                    P.op("act", lambda e, mt=mt, half=half, bk=bk: e.copy(out=kvout[:, mt, half * 256:(half + 1) * 256], in_=ps[:, bk, 0:256]),
                         reads=[b_ps[bk]], pwrites=[b_kvout])
                    if which == 1:
                        P.op("dve", lambda e, mt=mt, half=half, bk=bk: e.tensor_copy(out=Vn[:, mt, half * 256:(half + 1) * 256], in_=ps[:, bk, 0:256]),
                             reads=[b_ps[bk]], pwrites=[b_Vn])
            dst = memk_p if which == 0 else memv_p
            o = P.op("sp", lambda e, dst=dst: e.dma_start(out=dst[l].rearrange("(a p) x -> p a x", p=128), in_=kvout[:, :, :]),
                     reads=[b_kvout], dma=True)
            out_dmas.append(o)

        def grp(t0, n):
            prenorm(t0, n, g_pre)
            for hh in range(4):
                wb, bw, _ = load_w([(xa_w_q[l], hh * 128, 128)], D)
                bk = bank()
                mm(ps[:, bk, 0:n], [(wb[:, k, 0:128], xn[:, k, 0:n]) for k in range(8)], [bw, b_xn], bk)
                P.op("act", lambda e, hh=hh, bk=bk: e.copy(out=qT[:, hh, 0:n], in_=ps[:, bk, 0:n]), reads=[b_ps[bk]], pwrites=[b_qT])
            cnt = 0
            if t0 < SEQ:
                for hh in range(4):
                    gens = []
                    for qi in range(n // 128):
                        def dst(view, bkf, hh=hh, qi=qi):
                            P.op("act", lambda e: e.copy(out=oT[:, hh, qi * 128:(qi + 1) * 128], in_=view[:, 0, :]), reads=[b_ps[bkf]], pwrites=[b_oT])
                        gens.append(flash_gen(128, [(qT[:, hh, qi * 128:(qi + 1) * 128], [b_qT])],
                                              [dict(nk=256, KT=[(KT[:, hh, :], [b_KT])], V=[(Vn[:, mb, hh * 128:(hh + 1) * 128], 128, [b_Vn]) for mb in range(2)])],
                                              128, scale, scs[qi % 4], dst))
                    run_gens(gens)
            else:
                for s in range(16):
                    for which, (src, dstb, bdst) in enumerate(((memk_in, sKb, b_sKb), (memv_in, sVb, b_sVb))):
                        P.op("sp", lambda e, s=s, src=src: e.dma_start(out=sKs[:, :, :], in_=src[l, s].rearrange("(a p) x -> p a x", p=128)),
                             writes=[b_sKs], dma=True)
                        cast(dstb[:, :, :], sKs[:, :, :], [b_sKs], dict(writes=[bdst]))
                    for hh in range(4):
                        bk = bank()
                        for mb in range(2):
                            tr_block(ps[:, bk, mb * 128:(mb + 1) * 128], sKb[:, mb, hh * 128:(hh + 1) * 128], 128, bk, [b_sKb])
                        P.op("act", lambda e, hh=hh, bk=bk: e.copy(out=sKT[:, hh, :], in_=ps[:, bk, 0:256]), reads=[b_ps[bk]], pwrites=[b_sKT])
                    gens = []
                    for hh in range(4):
                        def dst(view, bkf, hh=hh, s=s):
                            P.op("act", lambda e: e.copy(out=oT[:, hh, s * 8:(s + 1) * 8], in_=view[:, 0, :]), reads=[b_ps[bkf]], pwrites=[b_oT])
                        gens.append(flash_gen(8, [(qT[:, hh, s * 8:(s + 1) * 8], [b_qT])],
                                              [dict(nk=256, KT=[(sKT[:, hh, :], [b_sKT])], V=[(sVb[:, mb, hh * 128:(hh + 1) * 128], 128, [b_sVb]) for mb in range(2)])],
                                              128, scale, scs[hh], dst))
                    run_gens(gens)
            out_linear(xa_w_o[l], 512, lambda k: oT[:, k, 0:n], [b_oT], n, fo, b_fo)
            postnorm(t0, n, g_post, 1.0, fo, b_fo)
        for (t0, n) in GROUPS:
            grp(t0, n)

    def convmod(l):
        j = l // 2
        ar_reset()
        g_pre, g_post = R_GAIN + l * 8 + 2, R_GAIN + l * 8 + 3
        extp = ar([128, 8, 2080], BF16); b_extp = [B(f"extp{c}") for c in range(8)]
        exts = ar([128, 8, 608], BF16); b_exts = [B(f"exts{c}") for c in range(8)]
        diag = ar([128, 31, 128], BF16); b_diag = B("diag")
        yb = ar([128, 8, 512]); b_y = B("y")
        utail = ar([128, 8, 32]); b_utail = B("utail")
        us = ar([128, 8, 128]); b_us = B("us")
        mu = ar([128, 512]); b_mu = B("mu")
        extsv = lambda c: exts[:, c, :].rearrange("p (s t) -> p s t", s=16)
        for c in range(8):
            P.op("pool", lambda e, c=c: e.memset(extp[:, c, 0:30], 0.0), pwrites=[b_extp[c]])
        for q in range(4):
            i = rot("tin")
            P.op("sp", lambda e, i=i, q=q: e.dma_start(out=tin[i][0:120, :], in_=cs_in[j][q * 4:(q + 1) * 4].rearrange("s r d -> (s r) d")),
                 writes=[b_tin[i]], dma=True)
            for half in range(2):
                bk = bank()
                for cc in range(4):
                    c = half * 4 + cc
                    tr_block(ps[:, bk, cc * 128:cc * 128 + 120], tin[i][0:120, c * 128:(c + 1) * 128], 120, bk, [b_tin[i]], f32=True)
                for cc in range(4):
                    c = half * 4 + cc
                    P.op("act", lambda e, c=c, cc=cc, bk=bk, q=q: e.copy(out=extsv(c)[:, q * 4:(q + 1) * 4, 0:30],
                                                                        in_=ps[:, bk, cc * 128:cc * 128 + 120].rearrange("p (s r) -> p s r", s=4)),
                         reads=[b_ps[bk]], pwrites=[b_exts[c]])
        o = P.op("sp", lambda e: e.dma_start(out=conv_s[j][:, 0:22, :], in_=cs_in[j][:, 8:30, :]), dma=True)
        out_dmas.append(o)

        def phaseA(t0, n):
            prenorm(t0, n, g_pre)
            w1 = conv_w_pw1[j]
            for c in range(8):
                wb, bw, _ = load_w([(w1, c * 128, 128), (w1, D + c * 128, 128)], D)
                ba = bank(); bg = bank()
                mm(ps[:, ba, 0:n], [(wb[:, k, 0:128], xn[:, k, 0:n]) for k in range(8)], [bw, b_xn], ba)
                mm(ps[:, bg, 0:n], [(wb[:, k, 128:256], xn[:, k, 0:n]) for k in range(8)], [bw, b_xn], bg)
                i = rot("tmpf")
                P.op("act", lambda e, i=i, bg=bg, c=c: e.activation(out=tmpf[i][:, 0:n], in_=ps[:, bg, 0:n], func=AF.Sigmoid,
                                                                   bias=vec(R_BPW1 + 2 * j + 1, c), scale=1.0),
                     reads=[b_ps[bg], b_vecs], writes=[b_tmpf[i]])
                if t0 < SEQ:
                    P.op("dve", lambda e, i=i, ba=ba, c=c: e.scalar_tensor_tensor(out=extp[:, c, 30 + t0:30 + t0 + n], in0=ps[:, ba, 0:n],
                                                                                  scalar=vec(R_BPW1 + 2 * j, c), in1=tmpf[i][:, 0:n],
                                                                                  op0=ALU.add, op1=ALU.mult),
                         reads=[b_ps[ba], b_vecs, b_tmpf[i]], pwrites=[b_extp[c]])
                    if t0 + n == SEQ:
                        P.op("dve", lambda e, i=i, ba=ba, c=c: e.scalar_tensor_tensor(out=utail[:, c, 0:30], in0=ps[:, ba, n - 30:n],
                                                                                      scalar=vec(R_BPW1 + 2 * j, c), in1=tmpf[i][:, n - 30:n],
                                                                                      op0=ALU.add, op1=ALU.mult),
                             reads=[b_ps[ba], b_vecs, b_tmpf[i]], pwrites=[b_utail])
                else:
                    P.op("dve", lambda e, i=i, ba=ba, c=c: e.scalar_tensor_tensor(out=us[:, c, :], in0=ps[:, ba, 0:n],
                                                                                  scalar=vec(R_BPW1 + 2 * j, c), in1=tmpf[i][:, 0:n],
                                                                                  op0=ALU.add, op1=ALU.mult),
                         reads=[b_ps[ba], b_vecs, b_tmpf[i]], pwrites=[b_us])
                    P.op("dve", lambda e, c=c: e.tensor_copy(out=extsv(c)[:, :, 30:38], in_=us[:, c, :].rearrange("p (s t) -> p s t", s=16)),
                         reads=[b_us], pwrites=[b_exts[c]])

        def phaseB(t0, n):
            for c in range(8):
                for k in range(31):
                    P.op("dve", lambda e, c=c, k=k: e.tensor_scalar(out=diag[:, k, :], in0=identb[:, :], scalar1=vec(R_WDW + j * 31 + k, c),
                                                                    scalar2=None, op0=ALU.mult),
                         reads=[b_const, b_vecs], pwrites=[b_diag])
                bk = bank()
                if t0 < SEQ:
                    items = [(ps[:, bk, 0:n], diag[:, k, :], extp[:, c, t0 + k:t0 + k + n]) for k in range(31)]
                    mmx(items, [b_diag, b_extp[c]], bk)
                else:
                    items = [(ps[:, bk, 0:n], diag[:, k, :], extsv(c)[:, :, k:k + 8]) for k in range(31)]
                    mmx(items, [b_diag, b_exts[c]], bk)
                P.op("act", lambda e, c=c, bk=bk: e.activation(out=yb[:, c, 0:n], in_=ps[:, bk, 0:n], func=AF.Identity,
                                                               bias=vec(R_BDW + j, c), scale=1.0),
                     reads=[b_ps[bk], b_vecs], pwrites=[b_y])
            b1 = stats_bc([(yb[:, c, 0:n], 128, [b_y]) for c in range(8)], n, square=False)
            P.op("act", lambda e: e.mul(out=mu[:, 0:n], in_=ps[:, b1, 0:n], mul=1.0 / D), reads=[b_ps[b1]], writes=[b_mu])
            b2 = stats_bc([(yb[:, c, 0:n], 128, [b_y]) for c in range(8)], n, square=True)
            i = rot("tmpf")
            P.op("dve", lambda e, i=i: e.tensor_tensor(out=tmpf[i][:, 0:n], in0=mu[:, 0:n], in1=mu[:, 0:n], op=ALU.mult), reads=[b_mu], writes=[b_tmpf[i]])
            ri = rot("rstd")
            P.op("dve", lambda e, i=i, ri=ri: e.scalar_tensor_tensor(out=rstd[ri][:, 0:n], in0=ps[:, b2, 0:n], scalar=1.0 / D, in1=tmpf[i][:, 0:n],
                                                                     op0=ALU.mult, op1=ALU.subtract),
                 reads=[b_ps[b2], b_tmpf[i]], writes=[b_rstd[ri]])
            P.op("act", lambda e, ri=ri: e.activation(out=rstd[ri][:, 0:n], in_=rstd[ri][:, 0:n], func=AF.Sqrt, scale=1.0, bias=LN_EPS),
                 reads=[b_rstd[ri]], writes=[b_rstd[ri]])
            P.op("dve", lambda e, ri=ri: e.reciprocal(out=rstd[ri][:, 0:n], in_=rstd[ri][:, 0:n]), reads=[b_rstd[ri]], writes=[b_rstd[ri]])
            for c in range(8):
                i = rot("tmpf")
                P.op("dve", lambda e, c=c, i=i: e.tensor_tensor(out=tmpf[i][:, 0:n], in0=yb[:, c, 0:n], in1=mu[:, 0:n], op=ALU.subtract),
                     reads=[b_y, b_mu], writes=[b_tmpf[i]])
                P.op("dve", lambda e, i=i, ri=ri: e.tensor_tensor(out=tmpf[i][:, 0:n], in0=tmpf[i][:, 0:n], in1=rstd[ri][:, 0:n], op=ALU.mult),
                     reads=[b_tmpf[i], b_rstd[ri]], writes=[b_tmpf[i]])
                P.op("act", lambda e, c=c, i=i: e.activation(out=xn[:, c, 0:n], in_=tmpf[i][:, 0:n], func=AF.Silu,
                                                             bias=vec(R_LNB + j, c), scale=vec(R_LNG + j, c)),
                     reads=[b_tmpf[i], b_vecs], pwrites=[b_xn])
            out_linear(conv_w_pw2[j], D, lambda k: xn[:, k, 0:n], [b_xn], n, yb, b_y, bias_row=R_BPW2 + j)
            postnorm(t0, n, g_post, 1.0, yb, b_y)

        for (t0, n) in GROUPS:
            phaseA(t0, n)
        i = rot("tin")
        for half in range(2):
            bk = bank()
            for cc in range(4):
                c = half * 4 + cc
                tr_block(ps[0:30, bk, cc * 128:(cc + 1) * 128], utail[:, c, 0:30], 128, bk, [b_utail], f32=True)
            P.op("act", lambda e, half=half, bk=bk, i=i: e.copy(out=tin[i][0:30, half * 512:(half + 1) * 512], in_=ps[0:30, bk, :]),
                 reads=[b_ps[bk]], pwrites=[b_tin[i]])
        o = P.op("sp", lambda e, i=i: e.dma_start(out=conv_p[j][:, :], in_=tin[i][0:30, :]), reads=[b_tin[i]], dma=True)
        out_dmas.append(o)
        i = rot("tin")
        for half in range(2):
            bk = bank()
            for cc in range(4):
                c = half * 4 + cc
                tr_block(ps[:, bk, cc * 128:(cc + 1) * 128], us[:, c, :], 128, bk, [b_us], f32=True)
            P.op("act", lambda e, half=half, bk=bk, i=i: e.copy(out=tin[i][:, half * 512:(half + 1) * 512], in_=ps[:, bk, :]),
                 reads=[b_ps[bk]], pwrites=[b_tin[i]])
        for s in range(16):
            o = P.op("sp", lambda e, i=i, s=s: e.dma_start(out=conv_s[j][s, 22:30, :], in_=tin[i][s * 8:(s + 1) * 8, :]), reads=[b_tin[i]], dma=True)
            out_dmas.append(o)
        for (t0, n) in GROUPS:
            phaseB(t0, n)

    def mla(l):
        j = l // 2
        ar_reset()
        g_pre, g_post = R_GAIN + l * 8 + 2, R_GAIN + l * 8 + 3
        scale = 96 ** -0.5
        w_in = mla_w_in[j]
        ckvS = ar([128, 2, 128], BF16); kpeS = ar([128, 128], BF16); b_keyS = B("keyS")
        mark0 = st["ar"]
        ckvT = ar([128, 2, SEQ], BF16); b_ckvT = B("ckvT")
        kpeT = ar([128, SEQ], BF16); b_kpeT = B("kpeT")
        Knat = ar([128, 16, 256], BF16); b_Knat = B("Knat")

        def alloc_work(nn):
            return dict(cs=ar([128, 2, nn]), b_cs=B("cs"), raw=ar([128, 3, nn]), b_raw=B("raw"),
                        cqT=ar([128, 3, nn], BF16), b_cqT=B("cqT"), wsw=ar([128, 8, 32], BF16), b_wsw=B("wsw"))

        def load_cs(W, t0, n):
            for w in range(2):
                P.op("sp", lambda e, w=w: e.dma_start(out=W["cs"][0:32, w, 0:n], in_=rope[w, :, t0:t0 + n]), pwrites=[W["b_cs"]], dma=True)

        def rope_apply(W, n, bx, bsw, out_bf, b_out, out_f=None, b_outf=None):
            cs_t, b_cs = W["cs"], W["b_cs"]
            i0 = rot("tmpf"); i1 = rot("tmpf")
            P.op("act", lambda e: e.copy(out=tmpf[i0][0:32, 0:n], in_=ps[0:32, bx, 0:n]), reads=[b_ps[bx]], writes=[b_tmpf[i0]])
            P.op("dve", lambda e: e.tensor_tensor(out=tmpf[i0][0:32, 0:n], in0=tmpf[i0][0:32, 0:n], in1=cs_t[0:32, 0, 0:n], op=ALU.mult),
                 reads=[b_tmpf[i0], b_cs], writes=[b_tmpf[i0]])
            P.op("dve", lambda e: e.tensor_tensor(out=tmpf[i1][0:32, 0:n], in0=ps[0:32, bsw, 0:n], in1=cs_t[0:32, 1, 0:n], op=ALU.mult),
                 reads=[b_ps[bsw], b_cs], writes=[b_tmpf[i1]])
            if out_f is not None:
                P.op("dve", lambda e: e.tensor_tensor(out=out_f, in0=tmpf[i0][0:32, 0:n], in1=tmpf[i1][0:32, 0:n], op=ALU.add),
                     reads=[b_tmpf[i0], b_tmpf[i1]], pwrites=[b_outf])
                P.op("act", lambda e: e.copy(out=out_bf, in_=out_f), reads=[b_outf], pwrites=[b_out])
            else:
                P.op("dve", lambda e: e.tensor_tensor(out=out_bf, in0=tmpf[i0][0:32, 0:n], in1=tmpf[i1][0:32, 0:n], op=ALU.add),
                     reads=[b_tmpf[i0], b_tmpf[i1]], pwrites=[b_out])

        def phaseA(W, t0, n):
            raw, b_raw, wsw, b_wsw = W["raw"], W["b_raw"], W["wsw"], W["b_wsw"]
            prompt = t0 < SEQ
            prenorm(t0, n, g_pre)
            load_cs(W, t0, n)
            wb, bw, _ = load_w([(w_in, 384, 288)], D)
            P.op("pool", lambda e: e.tensor_scalar(out=wsw[:, :, 0:16], in0=wb[:, :, 272:288], scalar1=-1.0, scalar2=None, op0=ALU.mult),
                 reads=[bw], pwrites=[b_wsw])
            P.op("pool", lambda e: e.tensor_copy(out=wsw[:, :, 16:32], in_=wb[:, :, 256:272]), reads=[bw], pwrites=[b_wsw])
            for c in range(2):
                bk = bank()
                mm(ps[:, bk, 0:n], [(wb[:, k, c * 128:(c + 1) * 128], xn[:, k, 0:n]) for k in range(8)], [bw, b_xn], bk)
                P.op("act", lambda e, c=c, bk=bk: e.copy(out=raw[:, c, 0:n], in_=ps[:, bk, 0:n]), reads=[b_ps[bk]], pwrites=[b_raw])
            bx = bank(); bs_ = bank()
            mm(ps[0:32, bx, 0:n], [(wb[:, k, 256:288], xn[:, k, 0:n]) for k in range(8)], [bw, b_xn], bx)
            mm(ps[0:32, bs_, 0:n], [(wsw[:, k, :], xn[:, k, 0:n]) for k in range(8)], [b_wsw, b_xn], bs_)
            if prompt:
                rope_apply(W, n, bx, bs_, kpeT[0:32, t0:t0 + n], b_kpeT, out_f=raw[0:32, 2, 0:n], b_outf=b_raw)
            else:
                rope_apply(W, n, bx, bs_, kpeS[0:32, 0:n], b_keyS, out_f=raw[0:32, 2, 0:n], b_outf=b_raw)
            bk = stats_bc([(raw[:, c, 0:n], 128, [b_raw]) for c in range(2)], n)
            ri = rstd_from(bk, n, 256, RMS_EPS)
            for c in range(2):
                P.op("dve", lambda e, c=c: e.scalar_tensor_tensor(out=raw[:, c, 0:n], in0=raw[:, c, 0:n], scalar=vec(R_KVN + j, c),
                                                                   in1=rstd[ri][:, 0:n], op0=ALU.mult, op1=ALU.mult),
                     reads=[b_raw, b_vecs, b_rstd[ri]], writes=[b_raw])
                if prompt:
                    P.op("act", lambda e, c=c: e.copy(out=ckvT[:, c, t0:t0 + n], in_=raw[:, c, 0:n]), reads=[b_raw], pwrites=[b_ckvT])
                else:
                    P.op("act", lambda e, c=c: e.copy(out=ckvS[:, c, 0:n], in_=raw[:, c, 0:n]), reads=[b_raw], pwrites=[b_keyS])
            lat_dst, kr_dst, r0 = (lat_p[j], kr_p[j], t0) if prompt else (lat_s[j], kr_s[j], 0)
            for tt in range(n // 128):
                i = rot("tin")
                bk = bank()
                for c in range(2):
                    tr_block(ps[:, bk, c * 128:(c + 1) * 128], raw[:, c, tt * 128:(tt + 1) * 128], 128, bk, [b_raw], f32=True)
                tr_block(ps[:, bk, 256:288], raw[0:32, 2, tt * 128:(tt + 1) * 128], 32, bk, [b_raw], f32=True)
                P.op("act", lambda e, i=i, bk=bk: e.copy(out=tin[i][:, 0:288], in_=ps[:, bk, 0:288]), reads=[b_ps[bk]], writes=[b_tin[i]])
                if prompt:
                    P.op("dve", lambda e, bk=bk, tt=tt: e.tensor_copy(out=Knat[:, t0 // 128 + tt, :], in_=ps[:, bk, 0:256]),
                         reads=[b_ps[bk]], pwrites=[b_Knat])
                o1 = P.op("sp", lambda e, i=i, tt=tt: e.dma_start(out=lat_dst[r0 + tt * 128:r0 + (tt + 1) * 128, :], in_=tin[i][:, 0:256]),
                          reads=[b_tin[i]], dma=True)
                o2 = P.op("sp", lambda e, i=i, tt=tt: e.dma_start(out=kr_dst[r0 + tt * 128:r0 + (tt + 1) * 128, :], in_=tin[i][:, 256:288]),
                          reads=[b_tin[i]], dma=True)
                out_dmas.extend([o1, o2])

        def phaseB(W, mark, t0, n):
            raw, b_raw, cqT, b_cqT = W["raw"], W["b_raw"], W["cqT"], W["b_cqT"]
            st["ar"] = mark
            P.barrier()
            prompt = t0 < SEQ
            wuq = ar([128, 3, 768], BF16); b_wuq = B("wuq")
            wukv = ar([128, 2, 1024], BF16); b_wukv = B("wukv")
            wukT = ar([128, 8, 256], BF16); b_wukT = B("wukT")
            wqsw = ar([128, 3, 256], BF16); b_wqsw = B("wqsw")
            qn = ar([128, n], BF16); b_qn = B("qn")
            qpe = ar([128, n], BF16); b_qpe = B("qpe")
            qlat = ar([128, 2, n], BF16); b_qlat = B("qlat")
            assert (st["ar"] - mark) * 4 >= 8 * n * 4, "fo alias region too small"
            nstream = 4 if prompt else 2
            olTs = [ar([128, 2, 128], BF16) for _ in range(nstream)]; b_olTs = [B(f"olT{i_}") for i_ in range(nstream)]
            oT = ar([128, 8, n], BF16); b_oT = B("oT")
            scs = flash_scratch(nstream)
            prenorm(t0, n, g_pre)
            load_cs(W, t0, n)
            wb, bw, _ = load_w([(w_in, 0, 256)], D)
            for c in range(2):
                bk = bank()
                mm(ps[:, bk, 0:n], [(wb[:, k, c * 128:(c + 1) * 128], xn[:, k, 0:n]) for k in range(8)], [bw, b_xn], bk)
                P.op("act", lambda e, c=c, bk=bk: e.copy(out=raw[:, c, 0:n], in_=ps[:, bk, 0:n]), reads=[b_ps[bk]], pwrites=[b_raw])
            wb, bw, _ = load_w([(w_in, 256, 128)], D)
            bk = bank()
            mm(ps[:, bk, 0:n], [(wb[:, k, 0:128], xn[:, k, 0:n]) for k in range(8)], [bw, b_xn], bk)
            P.op("act", lambda e, bk=bk: e.copy(out=raw[:, 2, 0:n], in_=ps[:, bk, 0:n]), reads=[b_ps[bk]], pwrites=[b_raw])
            bk = stats_bc([(raw[:, c, 0:n], 128, [b_raw]) for c in range(3)], n)
            ri = rstd_from(bk, n, 384, RMS_EPS)
            for c in range(3):
                P.op("dve", lambda e, c=c: e.scalar_tensor_tensor(out=cqT[:, c, 0:n], in0=raw[:, c, 0:n], scalar=vec(R_QN + j, c),
                                                                   in1=rstd[ri][:, 0:n], op0=ALU.mult, op1=ALU.mult),
                     reads=[b_raw, b_vecs, b_rstd[ri]], pwrites=[b_cqT])

            def load_head_weights(hb, need_q=True):
                load_w_to([(mla_w_ukv[j], hb * 1024, 1024)], 256, wukv[:, :, :], b_wukv)
                if not need_q:
                    return
                for q2 in range(2):
                    load_w_to([(mla_w_uq[j], hb * 768 + q2 * 384, 384)], 384, wuq[:, :, q2 * 384:(q2 + 1) * 384], b_wuq)
                wuq4 = lambda c: wuq[:, c, :].rearrange("p (h d) -> p h d", h=8)
                wqsw4 = lambda c: wqsw[:, c, :].rearrange("p (h d) -> p h d", h=8)
                for c in range(3):
                    P.op("pool", lambda e, c=c: e.tensor_scalar(out=wqsw4(c)[:, :, 0:16], in0=wuq4(c)[:, :, 80:96], scalar1=-1.0, scalar2=None, op0=ALU.mult),
                         reads=[b_wuq], pwrites=[b_wqsw])
                    P.op("pool", lambda e, c=c: e.tensor_copy(out=wqsw4(c)[:, :, 16:32], in_=wuq4(c)[:, :, 64:80]), reads=[b_wuq], pwrites=[b_wqsw])
                for hl in range(8):
                    bk = bank()
                    for lc in range(2):
                        tr_block(ps[0:64, bk, lc * 128:(lc + 1) * 128], wukv[:, lc, hl * 128:hl * 128 + 64], 128, bk, [b_wukv])
                    P.op("act", lambda e, hl=hl, bk=bk: e.copy(out=wukT[0:64, hl, :], in_=ps[0:64, bk, 0:256]), reads=[b_ps[bk]], pwrites=[b_wukT])

            if not prompt:
                qlat_all = ar([128, 2, 2048], BF16); b_qla = B("qlat_all")
                qpe_all = ar([128, 2048], BF16); b_qpa = B("qpe_all")
                olat_all = ar([128, 2, 2048], BF16); b_ola = B("olat_all")
            cnt = 0
            for hd in range(16):
                hb, hl = hd // 8, hd % 8
                if hl == 0:
                    load_head_weights(hb)
                bn = bank(); bx = bank(); bs_ = bank()
                mm(ps[0:64, bn, 0:n], [(wuq[:, c, hl * 96:hl * 96 + 64], cqT[:, c, 0:n]) for c in range(3)], [b_wuq, b_cqT], bn)
                mm(ps[0:32, bx, 0:n], [(wuq[:, c, hl * 96 + 64:hl * 96 + 96], cqT[:, c, 0:n]) for c in range(3)], [b_wuq, b_cqT], bx)
                mm(ps[0:32, bs_, 0:n], [(wqsw[:, c, hl * 32:(hl + 1) * 32], cqT[:, c, 0:n]) for c in range(3)], [b_wqsw, b_cqT], bs_)
                P.op("act", lambda e, bn=bn: e.copy(out=qn[0:64, 0:n], in_=ps[0:64, bn, 0:n]), reads=[b_ps[bn]], writes=[b_qn])
                if prompt:
                    rope_apply(W, n, bx, bs_, qpe[0:32, 0:n], b_qpe)
                else:
                    rope_apply(W, n, bx, bs_, qpe_all[0:32, hd * 128:(hd + 1) * 128], b_qpa)
                for lc in range(2):
                    bk = bank()
                    mm(ps[:, bk, 0:n], [(wukT[0:64, hl, lc * 128:(lc + 1) * 128], qn[0:64, 0:n])], [b_wukT, b_qn], bk)
                    if prompt:
                        P.op("act", lambda e, lc=lc, bk=bk: e.copy(out=qlat[:, lc, 0:n], in_=ps[:, bk, 0:n]), reads=[b_ps[bk]], pwrites=[b_qlat])
                    else:
                        P.op("act", lambda e, lc=lc, bk=bk, hd=hd: e.copy(out=qlat_all[:, lc, hd * 128:(hd + 1) * 128], in_=ps[:, bk, 0:n]),
                             reads=[b_ps[bk]], pwrites=[b_qla])
                if not prompt:
                    continue
                gens = []
                for qi in range(n // 128):
                    T = t0 // 128 + qi
                    nkeys = (T + 1) * 128
                    chunks = []
                    for c0 in range(0, nkeys, 512):
                        nk = min(512, nkeys - c0)
                        ch = dict(nk=nk, KT=[(ckvT[:, 0, c0:c0 + nk], [b_ckvT]), (ckvT[:, 1, c0:c0 + nk], [b_ckvT]), (kpeT[0:32, c0:c0 + nk], [b_kpeT])],
                                  V=[(Knat[:, c0 // 128 + kb, :], 128, [b_Knat]) for kb in range(nk // 128)])
                        if c0 + nk == nkeys:
                            ch["mask"] = (nk - 128, maskb[:, 0:128], 128)
                        chunks.append(ch)
                    qs = slice(qi * 128, (qi + 1) * 128)

                    def dst(view, bkf, hd=hd, hl=hl, qi=qi):
                        olT, b_olT = olTs[qi % nstream], b_olTs[qi % nstream]
                        P.op("act", lambda e: e.copy(out=olT[:, :, :], in_=view), reads=[b_ps[bkf]], writes=[b_olT])
                        yield
                        bo = bank()
                        po = (hd % 2) * 64
                        mmx([(ps[po:po + 64, bo, 0:128], wukv[:, lc, hl * 128 + 64:hl * 128 + 128], olT[:, lc, :]) for lc in range(2)], [b_wukv, b_olT], bo)
                        P.op("act", lambda e: e.copy(out=oT[po:po + 64, hd // 2, qi * 128:(qi + 1) * 128], in_=ps[po:po + 64, bo, 0:128]),
                             reads=[b_ps[bo]], pwrites=[b_oT])
                    gens.append(flash_gen(128, [(qlat[:, 0, qs], [b_qlat]), (qlat[:, 1, qs], [b_qlat]), (qpe[0:32, qs], [b_qpe])], chunks, 256, scale,
                                          scs[qi % nstream], dst))
                run_gens(gens)
            if not prompt:
                sample_attn(scs, qlat_all, b_qla, qpe_all, b_qpa, olat_all, b_ola)
                for hb in range(2):
                    load_head_weights(hb, need_q=False)
                    for hl in range(8):
                        hd = hb * 8 + hl
                        bo = bank()
                        po = (hd % 2) * 64
                        mmx([(ps[po:po + 64, bo, 0:128], wukv[:, lc, hl * 128 + 64:hl * 128 + 128], olat_all[:, lc, hd * 128:(hd + 1) * 128]) for lc in range(2)],
                            [b_wukv, b_ola], bo)
                        P.op("act", lambda e, hd=hd, bo=bo, po=po: e.copy(out=oT[po:po + 64, hd // 2, 0:128], in_=ps[po:po + 64, bo, 0:128]),
                             reads=[b_ps[bo]], pwrites=[b_oT])
            st_save = st["ar"]
            P.barrier()
            st["ar"] = mark
            fo = ar([128, 8, n]); b_fo = B("fo")
            st["ar"] = st_save
            out_linear(mla_w_o[j], D, lambda k: oT[:, k, 0:n], [b_oT], n, fo, b_fo)
            postnorm(t0, n, g_post, 1.0, fo, b_fo)

        def sample_attn(scs, qlat_all, b_qla, qpe_all, b_qpa, olat_all, b_ola):
            pti = [ar([128, 16], I32) for _ in range(2)]; ptf = [ar([128, 16]) for _ in range(2)]; idx = [ar([128, 16], I32) for _ in range(2)]
            b_pti = [B("pti0"), B("pti1")]; b_ptf = [B("ptf0"), B("ptf1")]; b_idx = [B("idx0"), B("idx1")]
            qs_lat = ar([128, 2, 128], BF16); b_qsl = B("qs_lat")
            qs_pe = ar([128, 128], BF16); b_qsp = B("qs_pe")
            stgL = [ar([128, 4, 256]) for _ in range(2)]; stgR = [ar([128, 4, 32]) for _ in range(2)]; b_stg = [B("stg0"), B("stg1")]
            KbL = [ar([128, 4, 256], BF16) for _ in range(2)]; KbR = [ar([128, 4, 32], BF16) for _ in range(2)]; b_Kb = [B("Kb0"), B("Kb1")]
            KTc = [ar([128, 2, 512], BF16) for _ in range(2)]; b_KTc = [B("KTc0"), B("KTc1")]
            KrT = [ar([128, 512], BF16) for _ in range(2)]; b_KrT = [B("KrT0"), B("KrT1")]
            vnew = ar([128, 256], BF16); b_vnew = B("vnew")
            latv = lat_pool[j].rearrange("(g r) d -> g (r d)", r=4)
            krv = kr_pool[j].rearrange("(g r) d -> g (r d)", r=4)
            cnt = [0]
            for s in range(16):
                si = s % 2
                sl = slice(s * 8, (s + 1) * 8)
                for pg in range(4):
                    P.op("sp", lambda e, s=s, si=si, pg=pg: e.dma_start(
                        out=pti[si][pg * 32:(pg + 1) * 32, :],
                        in_=pt_in[s, :].rearrange("(cc pg) -> pg cc", pg=4)[pg].partition_broadcast(32), allow_slow_non_contiguous=True),
                        pwrites=[b_pti[si]], dma=True)
                P.op("dve", lambda e, si=si: e.tensor_copy(out=ptf[si][:, :], in_=pti[si][:, :]), reads=[b_pti[si]], writes=[b_ptf[si]])
                P.op("dve", lambda e, si=si: e.tensor_scalar(out=ptf[si][:, :], in0=ptf[si][:, :], scalar1=32.0, scalar2=iot[:, 1:2],
                                                             op0=ALU.mult, op1=ALU.add),
                     reads=[b_ptf[si], b_const], writes=[b_ptf[si]])
                P.op("dve", lambda e, si=si: e.tensor_copy(out=idx[si][:, :], in_=ptf[si][:, :]), reads=[b_ptf[si]], writes=[b_idx[si]])
                for lc in range(2):
                    P.op("dve", lambda e, lc=lc, sl=sl: e.tensor_copy(out=qs_lat[:, lc, :].rearrange("p (h t) -> p h t", h=16),
                                                                      in_=qlat_all[:, lc, :].rearrange("p (h t) -> p h t", h=16)[:, :, sl]),
                         reads=[b_qla], pwrites=[b_qsl])
                P.op("dve", lambda e, sl=sl: e.tensor_copy(out=qs_pe[0:32, :].rearrange("p (h t) -> p h t", h=16),
                                                           in_=qpe_all[0:32, :].rearrange("p (h t) -> p h t", h=16)[:, :, sl]),
                     reads=[b_qpa], writes=[b_qsp])
                bk = bank()
                for lc in range(2):
                    tr_block(ps[0:8, bk, lc * 128:(lc + 1) * 128], ckvS[:, lc, sl], 128, bk, [b_keyS])
                P.op("act", lambda e, bk=bk: e.copy(out=vnew[0:8, :], in_=ps[0:8, bk, 0:256]), reads=[b_ps[bk]], writes=[b_vnew])

                def make_chunk(cc, s=s, si=si):
                    bi = cnt[0] % 2
                    cnt[0] += 1
                    P.op("pool", lambda e, bi=bi, cc=cc: e.indirect_dma_start(
                        out=stgL[bi][:, :, :].rearrange("p r d -> p (r d)"), out_offset=None, in_=latv,
                        in_offset=bass.IndirectOffsetOnAxis(ap=idx[si][:, cc:cc + 1], axis=0)),
                        reads=[b_idx[si]], pwrites=[b_stg[bi]], dma=True)
                    P.op("pool", lambda e, bi=bi, cc=cc: e.indirect_dma_start(
                        out=stgR[bi][:, :, :].rearrange("p r d -> p (r d)"), out_offset=None, in_=krv,
                        in_offset=bass.IndirectOffsetOnAxis(ap=idx[si][:, cc:cc + 1], axis=0)),
                        reads=[b_idx[si]], pwrites=[b_stg[bi]], dma=True)
                    cast(KbL[bi][:, :, :], stgL[bi][:, :, :], [b_stg[bi]], dict(writes=[b_Kb[bi]]))
                    P.op("pool", lambda e, bi=bi: e.tensor_copy(out=KbR[bi][:, :, :], in_=stgR[bi][:, :, :]), reads=[b_stg[bi]], pwrites=[b_Kb[bi]])
                    for lc in range(2):
                        bk = bank()
                        for r in range(4):
                            tr_block(ps[:, bk, r * 128:(r + 1) * 128], KbL[bi][:, r, lc * 128:(lc + 1) * 128], 128, bk, [b_Kb[bi]])
                        if lc:
                            P.op("act", lambda e, bi=bi, lc=lc, bk=bk: e.copy(out=KTc[bi][:, lc, :], in_=ps[:, bk, :]), reads=[b_ps[bk]], pwrites=[b_KTc[bi]])
                        else:
                            P.op("dve", lambda e, bi=bi, lc=lc, bk=bk: e.tensor_copy(out=KTc[bi][:, lc, :], in_=ps[:, bk, :]), reads=[b_ps[bk]], pwrites=[b_KTc[bi]])
                    bk = bank()
                    for r in range(4):
                        tr_block(ps[0:32, bk, r * 128:(r + 1) * 128], KbR[bi][:, r, :], 128, bk, [b_Kb[bi]])
                    P.op("act", lambda e, bi=bi, bk=bk: e.copy(out=KrT[bi][0:32, :], in_=ps[0:32, bk, :]), reads=[b_ps[bk]], writes=[b_KrT[bi]])
                    return dict(nk=512, KT=[(KTc[bi][:, 0, :], [b_KTc[bi]]), (KTc[bi][:, 1, :], [b_KTc[bi]]), (KrT[bi][0:32, :], [b_KrT[bi]])],
                                V=[(KbL[bi][:, r, :], 128, [b_Kb[bi]]) for r in range(4)])
                chunks = [(lambda cc=cc: make_chunk(cc)) for cc in range(16)]
                chunks.append(dict(nk=8, KT=[(ckvS[:, 0, sl], [b_keyS]), (ckvS[:, 1, sl], [b_keyS]), (kpeS[0:32, sl], [b_keyS])],
                                   V=[(vnew[0:8, :], 8, [b_vnew])], mask=(0, maskb[:, 128:136], 8)))

                def dst(view, bkf, s=s):
                    for lc in range(2):
                        P.op("act", lambda e, lc=lc: e.copy(out=olat_all[:, lc, :].rearrange("p (h t) -> p h t", h=16)[:, :, s * 8:(s + 1) * 8],
                                                            in_=view[:, lc, :].rearrange("p (h t) -> p h t", h=16)),
                             reads=[b_ps[bkf]], pwrites=[b_ola])
                flash(128, [(qs_lat[:, 0, :], [b_qsl]), (qs_lat[:, 1, :], [b_qsl]), (qs_pe[0:32, :], [b_qsp])], chunks, 256, scale, scs[s % 2], dst)

        W = alloc_work(512)
        mark = st["ar"]
        for (t0, n) in GROUPS:
            phaseA(W, t0, n)
        for (t0, n) in GROUPS:
            if t0 < SEQ:
                phaseB(W, mark, t0, n)
        P.barrier()
        st["ar"] = mark0
        W2 = alloc_work(128)
        mark2 = st["ar"]
        phaseB(W2, mark2, SEQ, NS)

    load_tokens(xp, SEQ, 0)
    load_tokens(xs, NS, SEQ)
    subs = []
    for l in range(4):
        subs.append(lambda l=l: ffn(l, 0, 0, 1))
        subs.append((lambda l=l: convmod(l)) if l % 2 == 0 else (lambda l=l: mla(l)))
        subs.append(lambda l=l: xattn(l))
        subs.append(lambda l=l: ffn(l, 1, 6, 7))
    for f_ in subs[:nsub]:
        f_()

    P.barrier()
    hsrc = lambda tokbase: [((lambda t, c=c: h[:, c, tokbase + t:tokbase + t + 128]), 128, b_h[c]) for c in range(8)]
    store_rows(y_p, hsrc(0), SEQ)
    store_rows(y_s, hsrc(SEQ), NS)
    fin = P.op("sp", None)
    fin.deps.update(out_dmas)
    P.emit()
    es.close()
    return nc


def _host_tables():
    f32 = np.float32
    consts = np.zeros((128, 512), f32)
    consts[:, 0:128] = np.eye(128, dtype=f32)
    q = np.arange(128)[:, None]; k = np.arange(128)[None, :]
    consts[:, 128:256] = np.where(k <= q, 0.0, -30000.0).astype(f32)
    t = (np.arange(128) % 8)[:, None]; tp = np.arange(8)[None, :]
    consts[:, 256:264] = np.where(tp <= t, 0.0, -30000.0).astype(f32)
    pos = np.concatenate([np.arange(SEQ), np.tile(8192 + np.arange(8), 16)]).astype(f32)
    inv_freq = (f32(10000.0) ** (-(np.arange(0, 32, 2, dtype=f32)) / f32(32))).astype(f32)
    ang = (pos[:, None] * inv_freq[None, :]).astype(f32)
    cos = np.cos(ang).astype(f32).T; sin = np.sin(ang).astype(f32).T
    rope = np.stack([np.concatenate([cos, cos], 0), np.concatenate([sin, sin], 0)], 0).astype(f32)
    iota = np.stack([np.arange(128), np.arange(128) % 32], 1).astype(f32)
    return consts, np.ascontiguousarray(rope), iota


def make_in_maps(inp, n_cores=8):
    f32 = np.float32
    A = lambda k: np.asarray(inp[k])
    vtab = np.zeros((128, D), f32)
    vtab[R_GAIN:R_GAIN + 32] = A("norm_gain").reshape(32, D)
    vtab[R_BPW1:R_BPW1 + 4] = A("conv_b_pw1").reshape(4, D)
    vtab[R_BDW:R_BDW + 2] = A("conv_b_dw"); vtab[R_LNG:R_LNG + 2] = A("conv_ln_g"); vtab[R_LNB:R_LNB + 2] = A("conv_ln_b")
    vtab[R_BPW2:R_BPW2 + 2] = A("conv_b_pw2"); vtab[R_MEMN:R_MEMN + 4] = A("xa_mem_norm")
    vtab[R_WDW:R_WDW + 62] = A("conv_w_dw").reshape(62, D)
    vtab[R_QN:R_QN + 2, 0:384] = A("mla_q_norm"); vtab[R_KVN:R_KVN + 2, 0:256] = A("mla_kv_norm")
    consts, rope, iota = _host_tables()
    C = np.ascontiguousarray
    n_pool = A("cache_mla_latent_l1").shape[0]
    shared = dict(vtab=vtab, consts=consts, rope=rope, iota=iota,
                  ffn_w_in=C(A("ffn_w_in"), dtype=f32), ffn_w_out=C(A("ffn_w_out"), dtype=f32),
                  lat1=A("cache_mla_latent_l1").reshape(n_pool * 128, 256), lat3=A("cache_mla_latent_l3").reshape(n_pool * 128, 256),
                  kr1=A("cache_mla_krope_l1").reshape(n_pool * 128, 32), kr3=A("cache_mla_krope_l3").reshape(n_pool * 128, 32),
                  conv_w_pw1=C(A("conv_w_pw1")), conv_w_pw2=C(A("conv_w_pw2")),
                  mla_w_in=C(A("mla_w_in")), mla_w_uq=A("mla_w_uq").reshape(2, 384, 1536), mla_w_ukv=A("mla_w_ukv").reshape(2, 256, 2048),
                  mla_w_o=C(A("mla_w_o")), xa_w_q=C(A("xa_w_q")), xa_w_kv=C(A("xa_w_kv")), xa_w_o=C(A("xa_w_o")))
    maps = []
    for c in range(n_cores):
        s0 = 16 * c
        m = dict(shared)
        m.update(xp=C(A("x_prompt")[c]), xs=C(A("x_sample")[s0:s0 + 16].reshape(NS, D)),
                 cs0=C(A("state_conv_l0")[s0:s0 + 16]), cs2=C(A("state_conv_l2")[s0:s0 + 16]),
                 memk=C(A("cache_mem_k")[:, s0:s0 + 16].reshape(4, 16, 256, 512)), memv=C(A("cache_mem_v")[:, s0:s0 + 16].reshape(4, 16, 256, 512)),
                 pt=C(A("page_table")[s0:s0 + 16].astype(np.int32)), memp=C(A("mem_prompt")[c]))
        maps.append(m)
    return maps, n_pool


def gather_outputs(r, n):
    f32 = np.float32
    g = lambda k: [np.asarray(r[c][k], f32) for c in range(n)]
    y_prompt = np.stack(g("y_p"), 0)
    y_sample = np.concatenate(g("y_s"), 0).reshape(16 * n, 8, D)
    conv0_p = np.stack(g("conv0_p"), 0); conv2_p = np.stack(g("conv2_p"), 0)
    lat1_p = np.stack(g("lat1_p"), 0); kr1_p = np.stack(g("kr1_p"), 0)
    lat3_p = np.stack(g("lat3_p"), 0); kr3_p = np.stack(g("kr3_p"), 0)
    memk = np.stack(g("memk_p"), 1).reshape(4, n, 256, 4, 128); memv = np.stack(g("memv_p"), 1).reshape(4, n, 256, 4, 128)
    conv0_s = np.concatenate(g("conv0_s"), 0); conv2_s = np.concatenate(g("conv2_s"), 0)
    lat1_s = np.concatenate(g("lat1_s"), 0).reshape(16 * n, 8, 256); kr1_s = np.concatenate(g("kr1_s"), 0).reshape(16 * n, 8, 32)
    lat3_s = np.concatenate(g("lat3_s"), 0).reshape(16 * n, 8, 256); kr3_s = np.concatenate(g("kr3_s"), 0).reshape(16 * n, 8, 32)
    return (y_prompt, y_sample, conv0_p, conv2_p, lat1_p, kr1_p, lat3_p, kr3_p, memk, memv,
            conv0_s, conv2_s, lat1_s, kr1_s, lat3_s, kr3_s)


def kernel(**inp):
    n = 8
    maps, n_pool = make_in_maps(inp, n)
    nc = build(nsub=16, n_pool=n_pool)
    res = run_bass_kernel_spmd(nc, maps, core_ids=list(range(n)))
    return gather_outputs(res.results, n)
```

```python
from concourse.bass_utils import run_bass_kernel_spmd
import numpy as np
import concourse.bass as bass
import concourse.mybir as mybir

F32 = mybir.dt.float32
BF16 = mybir.dt.bfloat16
I32 = mybir.dt.int32
AF = mybir.ActivationFunctionType
ALU = mybir.AluOpType
AX = mybir.AxisListType

ENGS = ("pe", "act", "dve", "pool", "sp")
SEM_EPOCH = 20000
N_DMA_SEMS = 24


class Buf:
    __slots__ = ("name", "writers", "readers", "war", "excl")

    def __init__(self, name, excl=False):
        self.name = name
        self.excl = excl
        self.writers = []
        self.readers = []
        self.war = []


class Op:
    __slots__ = ("eng", "fn", "deps", "idx", "sig", "cnt", "dma", "dsem", "dval", "dprev")

    def __init__(self, eng, fn):
        self.eng = eng
        self.fn = fn
        self.deps = set()
        self.sig = False
        self.dma = False


class Prog:
    def __init__(self, nc):
        self.nc = nc
        self.ops = []
        self.streams = {e: [] for e in ENGS}
        self.bar = {e: None for e in ENGS}

    def op(self, eng, fn, reads=(), writes=(), pwrites=(), dma=False):
        o = Op(eng, fn)
        o.dma = dma
        o.idx = len(self.ops)
        for b in reads:
            o.deps.update(b.writers)
            if b.excl:
                o.deps.update(b.readers)
        for b in writes:
            o.deps.update(b.readers)
            o.deps.update(b.writers)
            o.deps.update(b.war)
        for b in pwrites:
            o.deps.update(b.readers)
            o.deps.update(b.war)
        for b in reads:
            b.readers.append(o)
        for b in writes:
            b.writers = [o]
            b.readers = []
            b.war = []
        for b in pwrites:
            if b.readers:
                b.war = b.readers
                b.readers = []
                b.writers = [o]
            else:
                b.writers.append(o)
        if self.bar[eng] is not None:
            o.deps.update(self.bar[eng])
            self.bar[eng] = None
        o.deps.discard(o)
        self.ops.append(o)
        self.streams[eng].append(o)
        return o

    def barrier(self):
        last = [st[-1] for st in self.streams.values() if st]
        for e in ENGS:
            prev = self.bar[e] or []
            self.bar[e] = list(prev) + last

    def emit(self):
        nc = self.nc
        ops = self.ops
        for o in ops:
            for d in o.deps:
                if d.eng == "pe" and o.eng == "pe":
                    continue
                d.sig = True
        cnt = {e: 0 for e in ENGS}
        ndma = {"sp": 0, "pool": 0, "act": 0}
        dma_hist = {}
        for o in ops:
            if o.dma:
                o.sig = True
        dma_last = {}
        for o in ops:
            if o.dma:
                k = ndma[o.eng]
                ndma[o.eng] += 1
                slot = k % N_DMA_SEMS
                key = (o.eng, slot)
                prev = dma_last.get(key, 0)
                o.dsem = key
                o.dprev = prev
                o.dval = prev + 1
                dma_last[key] = prev + 1
            elif o.sig:
                cnt[o.eng] += 1
                o.cnt = cnt[o.eng]
        self._sem_cms = []
        def newsem(name):
            cm = nc.semaphore(name)
            s = cm.__enter__()
            self._sem_cms.append(cm)
            return s
        csem = {}
        for e in ("pe", "act", "dve", "pool"):
            n = cnt[e] // SEM_EPOCH + 1
            csem[e] = [newsem(f"c_{e}_{i}") for i in range(n)]
        dsem = {}
        for (e, slot) in dma_last:
            dsem[(e, slot)] = newsem(f"d_{e}_{slot}")
        self.csem, self.dsem = csem, dsem

        def target(d):
            if d.dma:
                return dsem[d.dsem], 16 * d.dval
            ep, v = divmod(d.cnt - 1, SEM_EPOCH)
            return csem[d.eng][ep], v + 1

        engobj = {"pe": nc.tensor, "act": nc.scalar, "dve": nc.vector, "pool": nc.gpsimd, "sp": nc.sync}

        def emit_stream(e, eng):
            seen = {}
            for o in self.streams[e]:
                need = {}
                for d in o.deps:
                    if d.eng == "pe" and e == "pe" and not d.dma:
                        continue
                    s, v = target(d)
                    k = id(s)
                    if seen.get(k, 0) >= v:
                        continue
                    if k not in need or need[k][1] < v:
                        need[k] = (s, v)
                if o.dma and o.dprev > 0:
                    s = dsem[o.dsem]
                    v = 16 * o.dprev
                    k = id(s)
                    if seen.get(k, 0) < v and (k not in need or need[k][1] < v):
                        need[k] = (s, v)
                for k, (s, v) in need.items():
                    eng.wait_ge(s, v)
                    seen[k] = v
                if o.fn is None:
                    continue
                ins = o.fn(eng)
                if o.dma:
                    ins.then_inc(dsem[o.dsem], 16)
                elif o.sig:
                    ep = (o.cnt - 1) // SEM_EPOCH
                    ins.then_inc(csem[e][ep], 1)

        with nc.Block() as block:
            @block.tensor
            def _(eng):
                emit_stream("pe", eng)

            @block.scalar
            def _(eng):
                emit_stream("act", eng)

            @block.vector
            def _(eng):
                emit_stream("dve", eng)

            @block.gpsimd
            def _(eng):
                emit_stream("pool", eng)

            @block.sync
            def _(eng):
                emit_stream("sp", eng)
        for cm in reversed(self._sem_cms):
            cm.__exit__(None, None, None)

from contextlib import ExitStack

D = 1024
SEQ = 2048
NS = 128
NT = SEQ + NS
DFF = 2816
NFF = DFF // 128
GROUPS = [(0, 512), (512, 512), (1024, 512), (1536, 512), (2048, 128)]
RMS_EPS = 1e-6
LN_EPS = 1e-5
ARENA_F32 = 19200

R_GAIN = 0
R_BPW1 = 32
R_BDW = 36
R_LNG = 38
R_LNB = 40
R_BPW2 = 42
R_MEMN = 44
R_WDW = 48
R_QN = 110
R_KVN = 112
NROWS = 114


def build(nsub=16, n_pool=10240):
    nc = bass.Bass("TRN2", target_bir_lowering=False)
    P = Prog(nc)
    es = ExitStack()

    def din(name, shape, dt=F32):
        return nc.dram_tensor(name, list(shape), dt, kind="ExternalInput").ap()

    def dout(name, shape, dt=F32):
        return nc.dram_tensor(name, list(shape), dt, kind="ExternalOutput").ap()

    def sb(name, shape, dt=F32):
        return es.enter_context(nc.sbuf_tensor(name, list(shape), dt))

    xp = din("xp", [SEQ, D]); xs = din("xs", [NS, D])
    vtab = din("vtab", [128, D])
    ffn_w_in = din("ffn_w_in", [4, 2, D, 2 * DFF]); ffn_w_out = din("ffn_w_out", [4, 2, DFF, D])
    consts = din("consts", [128, 512])
    rope = din("rope", [2, 32, NT])
    iota = din("iota", [128, 2])
    cs_in = [din("cs0", [16, 30, D]), din("cs2", [16, 30, D])]
    lat_pool = [din("lat1", [n_pool * 128, 256]), din("lat3", [n_pool * 128, 256])]
    kr_pool = [din("kr1", [n_pool * 128, 32]), din("kr3", [n_pool * 128, 32])]
    memk_in = din("memk", [4, 16, 256, 512]); memv_in = din("memv", [4, 16, 256, 512])
    pt_in = din("pt", [16, 64], I32)
    memp = din("memp", [256, D])
    conv_w_pw1 = din("conv_w_pw1", [2, D, 2 * D]); conv_w_pw2 = din("conv_w_pw2", [2, D, D])
    mla_w_in = din("mla_w_in", [2, D, 672]); mla_w_uq = din("mla_w_uq", [2, 384, 1536])
    mla_w_ukv = din("mla_w_ukv", [2, 256, 2048]); mla_w_o = din("mla_w_o", [2, D, D])
    xa_w_q = din("xa_w_q", [4, D, 512]); xa_w_kv = din("xa_w_kv", [4, D, D]); xa_w_o = din("xa_w_o", [4, 512, D])
    y_p = dout("y_p", [SEQ, D]); y_s = dout("y_s", [NS, D])
    conv_p = [dout("conv0_p", [30, D]), dout("conv2_p", [30, D])]
    lat_p = [dout("lat1_p", [SEQ, 256]), dout("lat3_p", [SEQ, 256])]
    kr_p = [dout("kr1_p", [SEQ, 32]), dout("kr3_p", [SEQ, 32])]
    memk_p = dout("memk_p", [4, 256, 512]); memv_p = dout("memv_p", [4, 256, 512])
    conv_s = [dout("conv0_s", [16, 30, D]), dout("conv2_s", [16, 30, D])]
    lat_s = [dout("lat1_s", [NS, 256]), dout("lat3_s", [NS, 256])]
    kr_s = [dout("kr1_s", [NS, 32]), dout("kr3_s", [NS, 32])]

    h = sb("h", [128, 8, NT])
    vecs = sb("vecs", [128, 8, 128])
    identf = sb("identf", [128, 128]); identb = sb("identb", [128, 128], BF16); onesb = sb("onesb", [128, 128], BF16)
    wst = [sb(f"wst{i}", [128, 2816]) for i in range(2)]
    wbf = [sb(f"wbf{i}", [128, 2816], BF16) for i in range(2)]
    xn = sb("xn", [128, 8, 512], BF16)
    rstd = [sb(f"rstd{i}", [128, 512]) for i in range(2)]
    tmpf = [sb(f"tmpf{i}", [128, 512]) for i in range(2)]
    sqb = [sb(f"sqb{i}", [128, 512], BF16) for i in range(2)]
    tin = [sb(f"tin{i}", [128, D]) for i in range(2)]
    arena = sb("arena", [128, ARENA_F32])
    ps = es.enter_context(nc.psum_tensor("ps", [128, 8, 512], F32))

    B = lambda n: Buf(n)
    b_h = [[B(f"h{c}_{t}") for t in range(NT // 128)] for c in range(8)]
    hb = lambda c, t0, n: b_h[c][t0 // 128:(t0 + n + 127) // 128]
    b_vecs = B("vecs"); b_const = B("const")
    b_wst = [B("wst0"), B("wst1")]; b_wbf = [B("wbf0"), B("wbf1")]
    b_xn = B("xn"); b_rstd = [B("rstd0"), B("rstd1")]; b_tmpf = [B("tmpf0"), B("tmpf1")]
    b_sqb = [B("sqb0"), B("sqb1")]; b_tin = [B("tin0"), B("tin1")]
    b_ps = [Buf(f"ps{i}", excl=True) for i in range(8)]
    st = {"bank": 0, "banks": list(range(8)), "w": 0, "rstd": 0, "tmpf": 0, "sqb": 0, "tin": 0, "ar": 0}

    bq = list(range(8))

    def bank():
        i = bq.pop(0)
        bq.append(i)
        return i

    def bank_hold():
        return bq.pop(0)

    def bank_release(i):
        bq.append(i)

    def rot(key, n=2):
        i = st[key] % n
        st[key] += 1
        return i

    def gidx(t0):
        return [g for g, (a, n) in enumerate(GROUPS) if a == t0][0]

    def ar_reset():
        P.barrier()
        st["ar"] = 0

    def ar(shape, dt=F32):
        n = int(np.prod(shape[1:]))
        words = (n + 1) // 2 if dt == BF16 else n
        a = st["ar"]
        st["ar"] += words
        assert st["ar"] <= ARENA_F32, ("arena overflow", st["ar"])
        v = arena[:, a:a + words]
        if dt != F32:
            v = v.bitcast(dt)[:, 0:n]
        if len(shape) == 3:
            v = v.rearrange("p (a b) -> p a b", a=shape[1])
        return v

    P.op("sp", lambda e: e.dma_start(out=identf[:], in_=consts[:, 0:128]), writes=[b_const], dma=True)
    P.op("dve", lambda e: e.tensor_copy(out=identb[:], in_=identf[:]), reads=[b_const], pwrites=[b_const])
    P.op("dve", lambda e: e.memset(onesb[:], 1.0), pwrites=[b_const])
    maskf = sb("maskf", [128, 136]); maskb = sb("maskb", [128, 136], BF16); iot = sb("iot", [128, 2])
    P.op("sp", lambda e: e.dma_start(out=maskf[:], in_=consts[:, 128:264]), pwrites=[b_const], dma=True)
    P.op("sp", lambda e: e.dma_start(out=iot[:], in_=iota[:, :]), pwrites=[b_const], dma=True)
    P.op("dve", lambda e: e.tensor_copy(out=maskb[:], in_=maskf[:]), reads=[b_const], pwrites=[b_const])

    def mm(out_ap, pairs, reads, bk):
        def fn(e, pairs=pairs, out_ap=out_ap):
            ins = None
            for i, (l, r) in enumerate(pairs):
                ins = e.matmul(out_ap, lhsT=l, rhs=r, start=(i == 0), stop=(i == len(pairs) - 1))
            return ins
        return P.op("pe", fn, reads=reads, writes=[b_ps[bk]])

    i = rot("tin")
    P.op("sp", lambda e: e.dma_start(out=tin[i][:], in_=vtab[:, :]), writes=[b_tin[i]], dma=True)
    for half in range(2):
        bk = bank()
        for cc in range(4):
            c = half * 4 + cc
            P.op("pe", lambda e, c=c, cc=cc, bk=bk: e.matmul(ps[:, bk, cc * 128:(cc + 1) * 128], lhsT=tin[i][:, c * 128:(c + 1) * 128],
                                                           rhs=identf[:], start=True, stop=True),
                 reads=[b_tin[i], b_const], pwrites=[b_ps[bk]])
        P.op("act", lambda e, half=half, bk=bk: e.copy(out=vecs[:, half * 4:half * 4 + 4, :],
                                                       in_=ps[:, bk, :].rearrange("p (a b) -> p a b", a=4)),
             reads=[b_ps[bk]], pwrites=[b_vecs])

    def vec(r, c, n=128):
        return vecs[0:n, c, r:r + 1]

    def load_tokens(src, ntok, tok0):
        for tt in range(ntok // 128):
            i = rot("tin")
            P.op("sp", lambda e, i=i, tt=tt: e.dma_start(out=tin[i][:], in_=src[tt * 128:(tt + 1) * 128, :]),
                 writes=[b_tin[i]], dma=True)
            t = tok0 + tt * 128
            for half in range(2):
                bk = bank()
                for cc in range(4):
                    c = half * 4 + cc
                    P.op("pe", lambda e, i=i, c=c, cc=cc, bk=bk: e.matmul(ps[:, bk, cc * 128:(cc + 1) * 128],
                                                                         lhsT=tin[i][:, c * 128:(c + 1) * 128], rhs=identf[:],
                                                                         start=True, stop=True),
                         reads=[b_tin[i], b_const], pwrites=[b_ps[bk]])
                P.op("act", lambda e, half=half, bk=bk, t=t: e.copy(out=h[:, half * 4:half * 4 + 4, t:t + 128],
                                                                   in_=ps[:, bk, :].rearrange("p (a b) -> p a b", a=4)),
                     reads=[b_ps[bk]], pwrites=[b_h[c][t // 128] for c in range(half * 4, half * 4 + 4)])

    out_dmas = []

    def store_rows(dst, srcs, ntok):
        for tt in range(ntok // 128):
            i = rot("tin")
            col = 0
            bk = None
            used = 0
            evs = []
            for (apf, msz, bufs) in srcs:
                if bk is None or used + msz > 512:
                    if bk is not None:
                        evs.append((bk, col - used, used))
                    bk = bank(); used = 0
                P.op("pe", lambda e, apf=apf, msz=msz, bk=bk, used=used, tt=tt: e.matmul(
                    ps[:, bk, used:used + msz], lhsT=apf(tt * 128), rhs=identf[0:msz, 0:msz], start=True, stop=True),
                    reads=list(bufs) + [b_const], pwrites=[b_ps[bk]])
                used += msz; col += msz
            evs.append((bk, col - used, used))
            for (bk, c0, n) in evs:
                P.op("act", lambda e, i=i, bk=bk, c0=c0, n=n: e.copy(out=tin[i][:, c0:c0 + n], in_=ps[:, bk, 0:n]),
                     reads=[b_ps[bk]], pwrites=[b_tin[i]])
            o = P.op("sp", lambda e, i=i, tt=tt, col=col: e.dma_start(out=dst[tt * 128:(tt + 1) * 128, 0:col], in_=tin[i][:, 0:col]),
                     reads=[b_tin[i]], dma=True)
            out_dmas.append(o)

    def stats_bc(srcs, n, square=True):
        bk = bank()
        for j, (ap, ksz, bufs) in enumerate(srcs):
            i = rot("sqb")
            if square:
                P.op("act", lambda e, i=i, ap=ap, ksz=ksz: e.activation(out=sqb[i][0:ksz, 0:n], in_=ap, func=AF.Square),
                     reads=bufs, writes=[b_sqb[i]])
            else:
                P.op("act", lambda e, i=i, ap=ap, ksz=ksz: e.copy(out=sqb[i][0:ksz, 0:n], in_=ap),
                     reads=bufs, writes=[b_sqb[i]])
            P.op("pe", lambda e, i=i, ksz=ksz, j=j, bk=bk, last=(j == len(srcs) - 1): e.matmul(
                ps[:, bk, 0:n], lhsT=onesb[0:ksz, :], rhs=sqb[i][0:ksz, 0:n], start=(j == 0), stop=last),
                reads=[b_sqb[i], b_const], pwrites=[b_ps[bk]])
        return bk

    def rstd_from(bk, n, dim, eps):
        i = rot("rstd")
        P.op("act", lambda e, i=i: e.activation(out=rstd[i][:, 0:n], in_=ps[:, bk, 0:n], func=AF.Sqrt, scale=1.0 / dim, bias=eps),
             reads=[b_ps[bk]], writes=[b_rstd[i]])
        P.op("dve", lambda e, i=i: e.reciprocal(out=rstd[i][:, 0:n], in_=rstd[i][:, 0:n]), reads=[b_rstd[i]], writes=[b_rstd[i]])
        return i

    def subtiles(n):
        return [(o, min(512, n - o)) for o in range(0, n, 512)]

    def prenorm(t0, n, grow, dst=None, b_dst=None):
        dst = xn if dst is None else dst
        b_dst = b_xn if b_dst is None else b_dst
        for (o, m) in subtiles(n):
            bk = stats_bc([(h[:, c, t0 + o:t0 + o + m], 128, hb(c, t0 + o, m)) for c in range(8)], m)
            ri = rstd_from(bk, m, D, RMS_EPS)
            for c in range(8):
                P.op("dve", lambda e, c=c, o=o, m=m, ri=ri: e.scalar_tensor_tensor(out=dst[:, c, o:o + m], in0=h[:, c, t0 + o:t0 + o + m],
                                                                                  scalar=vec(grow, c), in1=rstd[ri][:, 0:m],
                                                                                  op0=ALU.mult, op1=ALU.mult),
                     reads=hb(c, t0 + o, m) + [b_vecs, b_rstd[ri]], pwrites=[b_dst])

    def postnorm(t0, n, grow, a, fo, b_fo):
        for (o, m) in subtiles(n):
            bk = stats_bc([(fo[:, c, o:o + m], 128, [b_fo]) for c in range(8)], m)
            ri = rstd_from(bk, m, D, RMS_EPS)
            for c in range(8):
                i = rot("tmpf")
                P.op("dve", lambda e, c=c, i=i, o=o, m=m, ri=ri: e.scalar_tensor_tensor(out=tmpf[i][:, 0:m], in0=fo[:, c, o:o + m], scalar=vec(grow, c),
                                                                                       in1=rstd[ri][:, 0:m], op0=ALU.mult, op1=ALU.mult),
                     reads=[b_fo, b_vecs, b_rstd[ri]], writes=[b_tmpf[i]])
                P.op("dve", lambda e, c=c, i=i, o=o, m=m: e.scalar_tensor_tensor(out=h[:, c, t0 + o:t0 + o + m], in0=tmpf[i][:, 0:m], scalar=float(a),
                                                                                in1=h[:, c, t0 + o:t0 + o + m], op0=ALU.mult, op1=ALU.add),
                     reads=[b_tmpf[i]] + hb(c, t0 + o, m), writes=hb(c, t0 + o, m))

    def cast(out_ap, in_ap, reads, wkw):
        st["cast"] = st.get("cast", 0) + 1
        if st["cast"] % 2:
            P.op("act", lambda e: e.copy(out=out_ap, in_=in_ap), reads=reads, **wkw)
        else:
            P.op("dve", lambda e: e.tensor_copy(out=out_ap, in_=in_ap), reads=reads, **wkw)

    def load_w(segs, K):
        nk = (K + 127) // 128
        kp = min(K, 128)
        tot = sum(s[2] for s in segs)
        assert nk * tot <= 2816
        i = rot("w")
        sv = wst[i][:, 0:nk * tot].rearrange("p (k m) -> p k m", k=nk)
        bv = wbf[i][:, 0:nk * tot].rearrange("p (k m) -> p k m", k=nk)
        off = 0
        for (w2, c0, ncol) in segs:
            src = w2[:, c0:c0 + ncol].rearrange("(k p) m -> p k m", p=kp)
            P.op("sp", lambda e, src=src, off=off, ncol=ncol: e.dma_start(out=sv[0:kp, :, off:off + ncol], in_=src),
                 pwrites=[b_wst[i]], dma=True)
            off += ncol
        cast(bv[0:kp], sv[0:kp], [b_wst[i]], dict(writes=[b_wbf[i]]))
        return bv, b_wbf[i], nk

    FGROUPS = [(0, 768), (768, 768), (1536, 640)]

    def ffn(l, j, gpre, gpost):
        ar_reset()
        fxn = ar([128, 8, 768], BF16); b_fxn = B("fxn")
        hid = ar([128, NFF, 768], BF16); b_hid = B("hid")
        fo = ar([128, 8, 768]); b_fo = B("fo")
        w_in = ffn_w_in[l, j]; w_out = ffn_w_out[l, j]
        def grp(t0, n):
            prenorm(t0, n, R_GAIN + l * 8 + gpre, fxn, b_fxn)
            for f in range(NFF):
                wb, bw, nk = load_w([(w_in, f * 128, 128), (w_in, DFF + f * 128, 128)], D)
                for (o, m) in subtiles(n):
                    bg = bank(); bu = bank()
                    mm(ps[:, bg, 0:m], [(wb[:, k, 0:128], fxn[:, k, o:o + m]) for k in range(8)], [bw, b_fxn], bg)
                    mm(ps[:, bu, 0:m], [(wb[:, k, 128:256], fxn[:, k, o:o + m]) for k in range(8)], [bw, b_fxn], bu)
                    i = rot("tmpf")
                    P.op("act", lambda e, i=i, bg=bg, m=m: e.activation(out=tmpf[i][:, 0:m], in_=ps[:, bg, 0:m], func=AF.Silu),
                         reads=[b_ps[bg]], writes=[b_tmpf[i]])
                    P.op("dve", lambda e, i=i, bu=bu, f=f, o=o, m=m: e.tensor_tensor(out=hid[:, f, o:o + m], in0=tmpf[i][:, 0:m], in1=ps[:, bu, 0:m], op=ALU.mult),
                         reads=[b_tmpf[i], b_ps[bu]], pwrites=[b_hid])
            for c in range(8):
                wb, bw, nk = load_w([(w_out, c * 128, 128)], DFF)
                for (o, m) in subtiles(n):
                    bo = bank()
                    mm(ps[:, bo, 0:m], [(wb[:, f, 0:128], hid[:, f, o:o + m]) for f in range(NFF)], [bw, b_hid], bo)
                    P.op("act", lambda e, c=c, bo=bo, o=o, m=m: e.copy(out=fo[:, c, o:o + m], in_=ps[:, bo, 0:m]), reads=[b_ps[bo]], pwrites=[b_fo])
            postnorm(t0, n, R_GAIN + l * 8 + gpost, 0.5, fo, b_fo)
        for (t0, n) in FGROUPS:
            grp(t0, n)

    def mmx(items, reads, bk):
        def fn(e, items=items):
            ins = None
            for i, (o_, l_, r_) in enumerate(items):
                ins = e.matmul(o_, lhsT=l_, rhs=r_, start=(i == 0), stop=(i == len(items) - 1))
            return ins
        return P.op("pe", fn, reads=reads, writes=[b_ps[bk]])

    def tr_block(out_ap, in_ap, kk, bk, reads, f32=False):
        idt = identf if f32 else identb
        return P.op("pe", lambda e: e.matmul(out_ap, lhsT=in_ap, rhs=idt[0:kk, 0:kk], start=True, stop=True),
                    reads=list(reads) + [b_const], pwrites=[b_ps[bk]])

    def load_w_to(segs, K, dst, b_dst):
        nk = (K + 127) // 128
        kp = min(K, 128)
        tot = sum(s_[2] for s_ in segs)
        assert nk * tot <= 2816
        i = rot("w")
        sv = wst[i][:, 0:nk * tot].rearrange("p (k m) -> p k m", k=nk)
        off = 0
        for (w2, c0, ncol) in segs:
            src = w2[:, c0:c0 + ncol].rearrange("(k p) m -> p k m", p=kp)
            P.op("sp", lambda e, src=src, off=off, ncol=ncol: e.dma_start(out=sv[0:kp, :, off:off + ncol], in_=src),
                 pwrites=[b_wst[i]], dma=True)
            off += ncol
        cast(dst, sv[0:kp], [b_wst[i]], dict(pwrites=[b_dst]))

    def out_linear(w2d, K, rhs_fn, rhs_bufs, n, fo, b_fo, bias_row=None):
        nk = (K + 127) // 128
        for c in range(8):
            wb, bw, _ = load_w([(w2d, c * 128, 128)], K)
            bo = bank()
            mm(ps[:, bo, 0:n], [(wb[:, k, 0:128], rhs_fn(k)) for k in range(nk)], [bw] + list(rhs_bufs), bo)
            if bias_row is None:
                P.op("act", lambda e, c=c, bo=bo: e.copy(out=fo[:, c, 0:n], in_=ps[:, bo, 0:n]), reads=[b_ps[bo]], pwrites=[b_fo])
            else:
                P.op("act", lambda e, c=c, bo=bo: e.activation(out=fo[:, c, 0:n], in_=ps[:, bo, 0:n], func=AF.Identity,
                                                               bias=vec(bias_row, c), scale=1.0),
                     reads=[b_ps[bo], b_vecs], pwrites=[b_fo])

    def flash_scratch(nsets=2):
        scs = []
        for i in range(nsets):
            Pb = ar([128, 512], BF16); bP = B("P")
            scs.append(dict(P=Pb, PT=ar([128, 4, 128], BF16), oacc=ar([128, 256]), on=Pb[:, 0:256],
                            st=ar([128, 16]), bP=bP, bPT=B("PT"), boacc=B("oacc"), bon=bP,
                            bst={k: B("st" + k) for k in ("m0", "m1", "negm", "rs", "l", "alpha", "rinv", "alpha1")}))
        return scs

    def run_gens(gens):
        live = list(gens)
        while live:
            nxt = []
            for g_ in live:
                try:
                    next(g_)
                    nxt.append(g_)
                except StopIteration:
                    pass
            live = nxt

    def flash(*a, **k):
        run_gens([flash_gen(*a, **k)])
    STC = {"m0": 0, "m1": 1, "negm": 2, "rs": 3, "l": 4, "alpha": 5, "rinv": 6, "alpha1": 7}

    def flash_gen(nq, QT, chunks, dv, scale, sc, dst_fn, sc2=None):
        stc = lambda k: sc["st"][0:nq, STC[k]:STC[k] + 1]
        bst = sc["bst"]
        qreads = sum([list(b_) for (_, b_) in QT], [])
        n_ch = len(chunks)
        made = {}
        pb = lambda ci: sc if (sc2 is None or ci % 2 == 0) else sc2
        alpha = lambda ci: "alpha" if ci % 2 == 0 else "alpha1"

        def get(i_):
            if i_ not in made:
                made[i_] = chunks[i_]() if callable(chunks[i_]) else chunks[i_]
            return made[i_]

        def emit_S(ch):
            nk = ch["nk"]
            bk = bank_hold()
            S = ps[0:nq, bk, 0:nk]
            items = [(S, q_, k_) for (q_, _), (k_, _) in zip(QT, ch["KT"])]
            reads = qreads + sum([list(b_) for (_, b_) in ch["KT"]], [])
            if ch.get("mask") is not None:
                c0, map_, w = ch["mask"]
                items.append((ps[0:nq, bk, c0:c0 + w], identb[0:nq, 0:nq], map_))
                reads.append(b_const)
            mmx(items, reads, bk)
            return bk, S
        Sq = {}

        def stats(ci):
            ch = get(ci)
            nk = ch["nk"]
            bk, S = Sq.pop(ci)
            first = ci == 0
            mnew, mold = ("m0", "m1") if ci % 2 == 0 else ("m1", "m0")
            Pb = pb(ci)
            P.op("dve", lambda e: e.reduce_max(out=stc("rinv"), in_=S, axis=AX.X), reads=[b_ps[bk]], writes=[bst["rinv"]])
            if first:
                P.op("dve", lambda e: e.tensor_scalar(out=stc(mnew), in0=stc("rinv"), scalar1=-float(scale), scalar2=None, op0=ALU.mult),
                     reads=[bst["rinv"]], writes=[bst[mnew]])
            else:
                P.op("dve", lambda e: e.tensor_scalar(out=stc(mnew), in0=stc("rinv"), scalar1=-float(scale), scalar2=stc(mold),
                                                      op0=ALU.mult, op1=ALU.min),
                     reads=[bst["rinv"], bst[mold]], writes=[bst[mnew]])
            acc = "l" if first else "rs"
            P.op("act", lambda e: e.activation(out=Pb["P"][0:nq, 0:nk], in_=S, func=AF.Exp, bias=stc(mnew),
                                               scale=float(scale), accum_out=stc(acc)),
                 reads=[b_ps[bk], bst[mnew]], writes=[Pb["bP"], bst[acc]])
            if not first:
                al = alpha(ci)
                P.op("act", lambda e: e.activation(out=stc(al), in_=stc(mold), func=AF.Exp, bias=stc(mnew), scale=-1.0),
                     reads=[bst[mold], bst[mnew]], writes=[bst[al]])
                P.op("act", lambda e: e.activation(out=stc("l"), in_=stc("l"), func=AF.Identity, bias=stc("rs"), scale=stc(al)),
                     reads=[bst["l"], bst[al], bst["rs"]], writes=[bst["l"]])
            bank_release(bk)

        def tail(ci):
            ch = get(ci)
            nk = ch["nk"]
            first = ci == 0
            Pb = pb(ci)
            bkt = bank()
            blks = [(kb * 128, min(128, nk - kb * 128)) for kb in range((nk + 127) // 128)]
            for kb, (k0, ksz) in enumerate(blks):
                tr_block(ps[0:ksz, bkt, kb * nq:(kb + 1) * nq], Pb["P"][0:nq, k0:k0 + ksz], nq, bkt, [Pb["bP"]])
            nb = len(blks)
            kmax = 128 if nb > 1 else blks[0][1]
            src = ps[0:kmax, bkt, 0:nb * nq].rearrange("p (a b) -> p a b", a=nb)
            if ci % 2:
                P.op("dve", lambda e: e.tensor_copy(out=Pb["PT"][0:kmax, 0:nb, 0:nq], in_=src), reads=[b_ps[bkt]], writes=[Pb["bPT"]])
            else:
                P.op("act", lambda e: e.copy(out=Pb["PT"][0:kmax, 0:nb, 0:nq], in_=src), reads=[b_ps[bkt]], writes=[Pb["bPT"]])
            yield
            bko = bank()
            items = [(ps[0:nq, bko, 0:dv], Pb["PT"][0:ksz, kb, 0:nq], ch["V"][kb][0]) for kb, (k0, ksz) in enumerate(blks)]
            reads = [Pb["bPT"]] + sum([list(v_[2]) for v_ in ch["V"]], [])
            mmx(items, reads, bko)
            if first:
                P.op("act", lambda e: e.copy(out=sc["oacc"][0:nq, 0:dv], in_=ps[0:nq, bko, 0:dv]), reads=[b_ps[bko]], writes=[sc["boacc"]])
            else:
                al = alpha(ci)
                P.op("dve", lambda e: e.scalar_tensor_tensor(out=sc["oacc"][0:nq, 0:dv], in0=sc["oacc"][0:nq, 0:dv], scalar=stc(al),
                                                             in1=ps[0:nq, bko, 0:dv], op0=ALU.mult, op1=ALU.add),
                     reads=[sc["boacc"], bst[al], b_ps[bko]], writes=[sc["boacc"]])

        if sc2 is None:
            Sq[0] = emit_S(get(0))
            for ci in range(n_ch):
                if ci + 1 < n_ch:
                    Sq[ci + 1] = emit_S(get(ci + 1))
                stats(ci)
                yield
                yield from tail(ci)
                yield
        else:
            Sq[0] = emit_S(get(0))
            stats(0)
            if n_ch > 1:
                Sq[1] = emit_S(get(1))
            for ci in range(n_ch):
                if ci + 1 < n_ch:
                    stats(ci + 1)
                yield from tail(ci)
                if ci + 2 < n_ch:
                    Sq[ci + 2] = emit_S(get(ci + 2))
                yield
        P.op("dve", lambda e: e.reciprocal(out=stc("rinv"), in_=stc("l")), reads=[bst["l"]], writes=[bst["rinv"]])
        P.op("dve", lambda e: e.tensor_scalar(out=sc["on"][0:nq, 0:dv], in0=sc["oacc"][0:nq, 0:dv], scalar1=stc("rinv"), scalar2=None, op0=ALU.mult),
             reads=[sc["boacc"], bst["rinv"]], writes=[sc["bon"]])
        yield
        bkf = bank()
        nb = dv // 128
        for blk in range(nb):
            tr_block(ps[:, bkf, blk * nq:(blk + 1) * nq], sc["on"][0:nq, blk * 128:(blk + 1) * 128], nq, bkf, [sc["bon"]])
        r_ = dst_fn(ps[:, bkf, 0:nb * nq].rearrange("p (a b) -> p a b", a=nb), bkf)
        if r_ is not None:
            yield from r_

    def xattn(l):
        ar_reset()
        g_pre, g_post = R_GAIN + l * 8 + 4, R_GAIN + l * 8 + 5
        scale = 128 ** -0.5
        memT = ar([128, 8, 256]); b_memT = B("memT")
        memn = ar([128, 8, 256], BF16); b_memn = B("memn")
        KT = ar([128, 4, 256], BF16); b_KT = B("KT")
        Vn = ar([128, 2, 512], BF16); b_Vn = B("Vn")
        kvout = ar([128, 2, 512]); b_kvout = B("kvout")
        qT = ar([128, 4, 512], BF16); b_qT = B("qT")
        oT = ar([128, 4, 512], BF16); b_oT = B("oT")
        fo = ar([128, 8, 512]); b_fo = B("fo")
        scs = flash_scratch(4)
        sKs = ar([128, 2, 512]); b_sKs = B("sKs")
        sKb = ar([128, 2, 512], BF16); b_sKb = B("sKb")
        sVb = ar([128, 2, 512], BF16); b_sVb = B("sVb")
        sKT = ar([128, 4, 256], BF16); b_sKT = B("sKT")
        for tt in range(2):
            i = rot("tin")
            P.op("sp", lambda e, i=i, tt=tt: e.dma_start(out=tin[i][:], in_=memp[tt * 128:(tt + 1) * 128, :]), writes=[b_tin[i]], dma=True)
            for half in range(2):
                bk = bank()
                for cc in range(4):
                    c = half * 4 + cc
                    tr_block(ps[:, bk, cc * 128:(cc + 1) * 128], tin[i][:, c * 128:(c + 1) * 128], 128, bk, [b_tin[i]], f32=True)
                P.op("act", lambda e, half=half, bk=bk, tt=tt: e.copy(out=memT[:, half * 4:half * 4 + 4, tt * 128:(tt + 1) * 128],
                                                                     in_=ps[:, bk, :].rearrange("p (a b) -> p a b", a=4)),
                     reads=[b_ps[bk]], pwrites=[b_memT])
        bk = stats_bc([(memT[:, c, :], 128, [b_memT]) for c in range(8)], 256)
        ri = rstd_from(bk, 256, D, RMS_EPS)
        for c in range(8):
            P.op("dve", lambda e, c=c: e.scalar_tensor_tensor(out=memn[:, c, :], in0=memT[:, c, :], scalar=vec(R_MEMN + l, c),
                                                               in1=rstd[ri][:, 0:256], op0=ALU.mult, op1=ALU.mult),
                 reads=[b_memT, b_vecs, b_rstd[ri]], pwrites=[b_memn])
        wkv = xa_w_kv[l]
        for hh in range(4):
            wb, bw, _ = load_w([(wkv, hh * 128, 128)], D)
            bk = bank()
            mm(ps[:, bk, 0:256], [(wb[:, k, 0:128], memn[:, k, :]) for k in range(8)], [bw, b_memn], bk)
            P.op("act", lambda e, hh=hh, bk=bk: e.copy(out=KT[:, hh, :], in_=ps[:, bk, 0:256]), reads=[b_ps[bk]], pwrites=[b_KT])
        for which in range(2):
            for half in range(2):
                wb, bw, _ = load_w([(wkv, which * 512 + half * 256, 256)], D)
                for mt in range(2):
                    bk = bank()
                    mm(ps[:, bk, 0:256], [(memn[:, k, mt * 128:(mt + 1) * 128], wb[:, k, 0:256]) for k in range(8)], [bw, b_memn], bk)
                    P.op("act", lambda e, mt=mt, half=half, bk=bk: e.copy(out=kvout[:, mt, half * 256:(half + 1) * 256], in_=ps[:, bk, 0:256]),
                         reads=[b_ps[bk]], pwrites=[b_kvout])
                    if which == 1:
                        P.op("dve", lambda e, mt=mt, half=half, bk=bk: e.tensor_copy(out=Vn[:, mt, half * 256:(half + 1) * 256], in_=ps[:, bk, 0:256]),
                             reads=[b_ps[bk]], pwrites=[b_Vn])
            dst = memk_p if which == 0 else memv_p
            o = P.op("sp", lambda e, dst=dst: e.dma_start(out=dst[l].rearrange("(a p) x -> p a x", p=128), in_=kvout[:, :, :]),
                     reads=[b_kvout], dma=True)
            out_dmas.append(o)

        def grp(t0, n):
            prenorm(t0, n, g_pre)
            for hh in range(4):
                wb, bw, _ = load_w([(xa_w_q[l], hh * 128, 128)], D)
                bk = bank()
                mm(ps[:, bk, 0:n], [(wb[:, k, 0:128], xn[:, k, 0:n]) for k in range(8)], [bw, b_xn], bk)
                P.op("act", lambda e, hh=hh, bk=bk: e.copy(out=qT[:, hh, 0:n], in_=ps[:, bk, 0:n]), reads=[b_ps[bk]], pwrites=[b_qT])
            cnt = 0
            if t0 < SEQ:
                for hh in range(4):
                    gens = []
                    for qi in range(n // 128):
                        def dst(view, bkf, hh=hh, qi=qi):
                            P.op("act", lambda e: e.copy(out=oT[:, hh, qi * 128:(qi + 1) * 128], in_=view[:, 0, :]), reads=[b_ps[bkf]], pwrites=[b_oT])
                        gens.append(flash_gen(128, [(qT[:, hh, qi * 128:(qi + 1) * 128], [b_qT])],
                                              [dict(nk=256, KT=[(KT[:, hh, :], [b_KT])], V=[(Vn[:, mb, hh * 128:(hh + 1) * 128], 128, [b_Vn]) for mb in range(2)])],
                                              128, scale, scs[qi % 4], dst))
                    run_gens(gens)
            else:
                for s in range(16):
                    for which, (src, dstb, bdst) in enumerate(((memk_in, sKb, b_sKb), (memv_in, sVb, b_sVb))):
                        P.op("sp", lambda e, s=s, src=src: e.dma_start(out=sKs[:, :, :], in_=src[l, s].rearrange("(a p) x -> p a x", p=128)),
                             writes=[b_sKs], dma=True)
                        cast(dstb[:, :, :], sKs[:, :, :], [b_sKs], dict(writes=[bdst]))
                    for hh in range(4):
                        bk = bank()
                        for mb in range(2):
                            tr_block(ps[:, bk, mb * 128:(mb + 1) * 128], sKb[:, mb, hh * 128:(hh + 1) * 128], 128, bk, [b_sKb])
                        P.op("act", lambda e, hh=hh, bk=bk: e.copy(out=sKT[:, hh, :], in_=ps[:, bk, 0:256]), reads=[b_ps[bk]], pwrites=[b_sKT])
                    gens = []
                    for hh in range(4):
                        def dst(view, bkf, hh=hh, s=s):
                            P.op("act", lambda e: e.copy(out=oT[:, hh, s * 8:(s + 1) * 8], in_=view[:, 0, :]), reads=[b_ps[bkf]], pwrites=[b_oT])
                        gens.append(flash_gen(8, [(qT[:, hh, s * 8:(s + 1) * 8], [b_qT])],
                                              [dict(nk=256, KT=[(sKT[:, hh, :], [b_sKT])], V=[(sVb[:, mb, hh * 128:(hh + 1) * 128], 128, [b_sVb]) for mb in range(2)])],
                                              128, scale, scs[hh], dst))
                    run_gens(gens)
            out_linear(xa_w_o[l], 512, lambda k: oT[:, k, 0:n], [b_oT], n, fo, b_fo)
            postnorm(t0, n, g_post, 1.0, fo, b_fo)
        for (t0, n) in GROUPS:
            grp(t0, n)

    def convmod(l):
        j = l // 2
        ar_reset()
        g_pre, g_post = R_GAIN + l * 8 + 2, R_GAIN + l * 8 + 3
        extp = ar([128, 8, 2080], BF16); b_extp = [B(f"extp{c}") for c in range(8)]
        exts = ar([128, 8, 608], BF16); b_exts = [B(f"exts{c}") for c in range(8)]
        diag = ar([128, 31, 128], BF16); b_diag = B("diag")
        yb = ar([128, 8, 512]); b_y = B("y")
        utail = ar([128, 8, 32]); b_utail = B("utail")
        us = ar([128, 8, 128]); b_us = B("us")
        mu = ar([128, 512]); b_mu = B("mu")
        extsv = lambda c: exts[:, c, :].rearrange("p (s t) -> p s t", s=16)
        for c in range(8):
            P.op("pool", lambda e, c=c: e.memset(extp[:, c, 0:30], 0.0), pwrites=[b_extp[c]])
        for q in range(4):
            i = rot("tin")
            P.op("sp", lambda e, i=i, q=q: e.dma_start(out=tin[i][0:120, :], in_=cs_in[j][q * 4:(q + 1) * 4].rearrange("s r d -> (s r) d")),
                 writes=[b_tin[i]], dma=True)
            for half in range(2):
                bk = bank()
                for cc in range(4):
                    c = half * 4 + cc
                    tr_block(ps[:, bk, cc * 128:cc * 128 + 120], tin[i][0:120, c * 128:(c + 1) * 128], 120, bk, [b_tin[i]], f32=True)
                for cc in range(4):
                    c = half * 4 + cc
                    P.op("act", lambda e, c=c, cc=cc, bk=bk, q=q: e.copy(out=extsv(c)[:, q * 4:(q + 1) * 4, 0:30],
                                                                        in_=ps[:, bk, cc * 128:cc * 128 + 120].rearrange("p (s r) -> p s r", s=4)),
                         reads=[b_ps[bk]], pwrites=[b_exts[c]])
        o = P.op("sp", lambda e: e.dma_start(out=conv_s[j][:, 0:22, :], in_=cs_in[j][:, 8:30, :]), dma=True)
        out_dmas.append(o)

        def phaseA(t0, n):
            prenorm(t0, n, g_pre)
            w1 = conv_w_pw1[j]
            for c in range(8):
                wb, bw, _ = load_w([(w1, c * 128, 128), (w1, D + c * 128, 128)], D)
                ba = bank(); bg = bank()
                mm(ps[:, ba, 0:n], [(wb[:, k, 0:128], xn[:, k, 0:n]) for k in range(8)], [bw, b_xn], ba)
                mm(ps[:, bg, 0:n], [(wb[:, k, 128:256], xn[:, k, 0:n]) for k in range(8)], [bw, b_xn], bg)
                i = rot("tmpf")
                P.op("act", lambda e, i=i, bg=bg, c=c: e.activation(out=tmpf[i][:, 0:n], in_=ps[:, bg, 0:n], func=AF.Sigmoid,
                                                                   bias=vec(R_BPW1 + 2 * j + 1, c), scale=1.0),
                     reads=[b_ps[bg], b_vecs], writes=[b_tmpf[i]])
                if t0 < SEQ:
                    P.op("dve", lambda e, i=i, ba=ba, c=c: e.scalar_tensor_tensor(out=extp[:, c, 30 + t0:30 + t0 + n], in0=ps[:, ba, 0:n],
                                                                                  scalar=vec(R_BPW1 + 2 * j, c), in1=tmpf[i][:, 0:n],
                                                                                  op0=ALU.add, op1=ALU.mult),
                         reads=[b_ps[ba], b_vecs, b_tmpf[i]], pwrites=[b_extp[c]])
                    if t0 + n == SEQ:
                        P.op("dve", lambda e, i=i, ba=ba, c=c: e.scalar_tensor_tensor(out=utail[:, c, 0:30], in0=ps[:, ba, n - 30:n],
                                                                                      scalar=vec(R_BPW1 + 2 * j, c), in1=tmpf[i][:, n - 30:n],
                                                                                      op0=ALU.add, op1=ALU.mult),
                             reads=[b_ps[ba], b_vecs, b_tmpf[i]], pwrites=[b_utail])
                else:
                    P.op("dve", lambda e, i=i, ba=ba, c=c: e.scalar_tensor_tensor(out=us[:, c, :], in0=ps[:, ba, 0:n],
                                                                                  scalar=vec(R_BPW1 + 2 * j, c), in1=tmpf[i][:, 0:n],
                                                                                  op0=ALU.add, op1=ALU.mult),
                         reads=[b_ps[ba], b_vecs, b_tmpf[i]], pwrites=[b_us])
                    P.op("dve", lambda e, c=c: e.tensor_copy(out=extsv(c)[:, :, 30:38], in_=us[:, c, :].rearrange("p (s t) -> p s t", s=16)),
                         reads=[b_us], pwrites=[b_exts[c]])

        def phaseB(t0, n):
            for c in range(8):
                for k in range(31):
                    P.op("dve", lambda e, c=c, k=k: e.tensor_scalar(out=diag[:, k, :], in0=identb[:, :], scalar1=vec(R_WDW + j * 31 + k, c),
                                                                    scalar2=None, op0=ALU.mult),
                         reads=[b_const, b_vecs], pwrites=[b_diag])
                bk = bank()
                if t0 < SEQ:
                    items = [(ps[:, bk, 0:n], diag[:, k, :], extp[:, c, t0 + k:t0 + k + n]) for k in range(31)]
                    mmx(items, [b_diag, b_extp[c]], bk)
                else:
                    items = [(ps[:, bk, 0:n], diag[:, k, :], extsv(c)[:, :, k:k + 8]) for k in range(31)]
                    mmx(items, [b_diag, b_exts[c]], bk)
                P.op("act", lambda e, c=c, bk=bk: e.activation(out=yb[:, c, 0:n], in_=ps[:, bk, 0:n], func=AF.Identity,
                                                               bias=vec(R_BDW + j, c), scale=1.0),
                     reads=[b_ps[bk], b_vecs], pwrites=[b_y])
            b1 = stats_bc([(yb[:, c, 0:n], 128, [b_y]) for c in range(8)], n, square=False)
            P.op("act", lambda e: e.mul(out=mu[:, 0:n], in_=ps[:, b1, 0:n], mul=1.0 / D), reads=[b_ps[b1]], writes=[b_mu])
            b2 = stats_bc([(yb[:, c, 0:n], 128, [b_y]) for c in range(8)], n, square=True)
            i = rot("tmpf")
            P.op("dve", lambda e, i=i: e.tensor_tensor(out=tmpf[i][:, 0:n], in0=mu[:, 0:n], in1=mu[:, 0:n], op=ALU.mult), reads=[b_mu], writes=[b_tmpf[i]])
            ri = rot("rstd")
            P.op("dve", lambda e, i=i, ri=ri: e.scalar_tensor_tensor(out=rstd[ri][:, 0:n], in0=ps[:, b2, 0:n], scalar=1.0 / D, in1=tmpf[i][:, 0:n],
                                                                     op0=ALU.mult, op1=ALU.subtract),
                 reads=[b_ps[b2], b_tmpf[i]], writes=[b_rstd[ri]])
            P.op("act", lambda e, ri=ri: e.activation(out=rstd[ri][:, 0:n], in_=rstd[ri][:, 0:n], func=AF.Sqrt, scale=1.0, bias=LN_EPS),
                 reads=[b_rstd[ri]], writes=[b_rstd[ri]])
            P.op("dve", lambda e, ri=ri: e.reciprocal(out=rstd[ri][:, 0:n], in_=rstd[ri][:, 0:n]), reads=[b_rstd[ri]], writes=[b_rstd[ri]])
            for c in range(8):
                i = rot("tmpf")
                P.op("dve", lambda e, c=c, i=i: e.tensor_tensor(out=tmpf[i][:, 0:n], in0=yb[:, c, 0:n], in1=mu[:, 0:n], op=ALU.subtract),
                     reads=[b_y, b_mu], writes=[b_tmpf[i]])
                P.op("dve", lambda e, i=i, ri=ri: e.tensor_tensor(out=tmpf[i][:, 0:n], in0=tmpf[i][:, 0:n], in1=rstd[ri][:, 0:n], op=ALU.mult),
                     reads=[b_tmpf[i], b_rstd[ri]], writes=[b_tmpf[i]])
                P.op("act", lambda e, c=c, i=i: e.activation(out=xn[:, c, 0:n], in_=tmpf[i][:, 0:n], func=AF.Silu,
                                                             bias=vec(R_LNB + j, c), scale=vec(R_LNG + j, c)),
                     reads=[b_tmpf[i], b_vecs], pwrites=[b_xn])
            out_linear(conv_w_pw2[j], D, lambda k: xn[:, k, 0:n], [b_xn], n, yb, b_y, bias_row=R_BPW2 + j)
            postnorm(t0, n, g_post, 1.0, yb, b_y)

        for (t0, n) in GROUPS:
            phaseA(t0, n)
        i = rot("tin")
        for half in range(2):
            bk = bank()
            for cc in range(4):
                c = half * 4 + cc
                tr_block(ps[0:30, bk, cc * 128:(cc + 1) * 128], utail[:, c, 0:30], 128, bk, [b_utail], f32=True)
            P.op("act", lambda e, half=half, bk=bk, i=i: e.copy(out=tin[i][0:30, half * 512:(half + 1) * 512], in_=ps[0:30, bk, :]),
                 reads=[b_ps[bk]], pwrites=[b_tin[i]])
        o = P.op("sp", lambda e, i=i: e.dma_start(out=conv_p[j][:, :], in_=tin[i][0:30, :]), reads=[b_tin[i]], dma=True)
        out_dmas.append(o)
        i = rot("tin")
        for half in range(2):
            bk = bank()
            for cc in range(4):
                c = half * 4 + cc
                tr_block(ps[:, bk, cc * 128:(cc + 1) * 128], us[:, c, :], 128, bk, [b_us], f32=True)
            P.op("act", lambda e, half=half, bk=bk, i=i: e.copy(out=tin[i][:, half * 512:(half + 1) * 512], in_=ps[:, bk, :]),
                 reads=[b_ps[bk]], pwrites=[b_tin[i]])
        for s in range(16):
            o = P.op("sp", lambda e, i=i, s=s: e.dma_start(out=conv_s[j][s, 22:30, :], in_=tin[i][s * 8:(s + 1) * 8, :]), reads=[b_tin[i]], dma=True)
            out_dmas.append(o)
        for (t0, n) in GROUPS:
            phaseB(t0, n)

    def mla(l):
        j = l // 2
        ar_reset()
        g_pre, g_post = R_GAIN + l * 8 + 2, R_GAIN + l * 8 + 3
        scale = 96 ** -0.5
        w_in = mla_w_in[j]
        ckvS = ar([128, 2, 128], BF16); kpeS = ar([128, 128], BF16); b_keyS = B("keyS")
        mark0 = st["ar"]
        ckvT = ar([128, 2, SEQ], BF16); b_ckvT = B("ckvT")
        kpeT = ar([128, SEQ], BF16); b_kpeT = B("kpeT")
        Knat = ar([128, 16, 256], BF16); b_Knat = B("Knat")

        def alloc_work(nn):
            return dict(cs=ar([128, 2, nn]), b_cs=B("cs"), raw=ar([128, 3, nn]), b_raw=B("raw"),
                        cqT=ar([128, 3, nn], BF16), b_cqT=B("cqT"), wsw=ar([128, 8, 32], BF16), b_wsw=B("wsw"))

        def load_cs(W, t0, n):
            for w in range(2):
                P.op("sp", lambda e, w=w: e.dma_start(out=W["cs"][0:32, w, 0:n], in_=rope[w, :, t0:t0 + n]), pwrites=[W["b_cs"]], dma=True)

        def rope_apply(W, n, bx, bsw, out_bf, b_out, out_f=None, b_outf=None):
            cs_t, b_cs = W["cs"], W["b_cs"]
            i0 = rot("tmpf"); i1 = rot("tmpf")
            P.op("act", lambda e: e.copy(out=tmpf[i0][0:32, 0:n], in_=ps[0:32, bx, 0:n]), reads=[b_ps[bx]], writes=[b_tmpf[i0]])
            P.op("dve", lambda e: e.tensor_tensor(out=tmpf[i0][0:32, 0:n], in0=tmpf[i0][0:32, 0:n], in1=cs_t[0:32, 0, 0:n], op=ALU.mult),
                 reads=[b_tmpf[i0], b_cs], writes=[b_tmpf[i0]])
            P.op("dve", lambda e: e.tensor_tensor(out=tmpf[i1][0:32, 0:n], in0=ps[0:32, bsw, 0:n], in1=cs_t[0:32, 1, 0:n], op=ALU.mult),
                 reads=[b_ps[bsw], b_cs], writes=[b_tmpf[i1]])
            if out_f is not None:
                P.op("dve", lambda e: e.tensor_tensor(out=out_f, in0=tmpf[i0][0:32, 0:n], in1=tmpf[i1][0:32, 0:n], op=ALU.add),
                     reads=[b_tmpf[i0], b_tmpf[i1]], pwrites=[b_outf])
                P.op("act", lambda e: e.copy(out=out_bf, in_=out_f), reads=[b_outf], pwrites=[b_out])
            else:
                P.op("dve", lambda e: e.tensor_tensor(out=out_bf, in0=tmpf[i0][0:32, 0:n], in1=tmpf[i1][0:32, 0:n], op=ALU.add),
                     reads=[b_tmpf[i0], b_tmpf[i1]], pwrites=[b_out])

        def phaseA(W, t0, n):
            raw, b_raw, wsw, b_wsw = W["raw"], W["b_raw"], W["wsw"], W["b_wsw"]
            prompt = t0 < SEQ
            prenorm(t0, n, g_pre)
            load_cs(W, t0, n)
            wb, bw, _ = load_w([(w_in, 384, 288)], D)
            P.op("pool", lambda e: e.tensor_scalar(out=wsw[:, :, 0:16], in0=wb[:, :, 272:288], scalar1=-1.0, scalar2=None, op0=ALU.mult),
                 reads=[bw], pwrites=[b_wsw])
            P.op("pool", lambda e: e.tensor_copy(out=wsw[:, :, 16:32], in_=wb[:, :, 256:272]), reads=[bw], pwrites=[b_wsw])
            for c in range(2):
                bk = bank()
                mm(ps[:, bk, 0:n], [(wb[:, k, c * 128:(c + 1) * 128], xn[:, k, 0:n]) for k in range(8)], [bw, b_xn], bk)
                P.op("act", lambda e, c=c, bk=bk: e.copy(out=raw[:, c, 0:n], in_=ps[:, bk, 0:n]), reads=[b_ps[bk]], pwrites=[b_raw])
            bx = bank(); bs_ = bank()
            mm(ps[0:32, bx, 0:n], [(wb[:, k, 256:288], xn[:, k, 0:n]) for k in range(8)], [bw, b_xn], bx)
            mm(ps[0:32, bs_, 0:n], [(wsw[:, k, :], xn[:, k, 0:n]) for k in range(8)], [b_wsw, b_xn], bs_)
            if prompt:
                rope_apply(W, n, bx, bs_, kpeT[0:32, t0:t0 + n], b_kpeT, out_f=raw[0:32, 2, 0:n], b_outf=b_raw)
            else:
                rope_apply(W, n, bx, bs_, kpeS[0:32, 0:n], b_keyS, out_f=raw[0:32, 2, 0:n], b_outf=b_raw)
            bk = stats_bc([(raw[:, c, 0:n], 128, [b_raw]) for c in range(2)], n)
            ri = rstd_from(bk, n, 256, RMS_EPS)
            for c in range(2):
                P.op("dve", lambda e, c=c: e.scalar_tensor_tensor(out=raw[:, c, 0:n], in0=raw[:, c, 0:n], scalar=vec(R_KVN + j, c),
                                                                   in1=rstd[ri][:, 0:n], op0=ALU.mult, op1=ALU.mult),
                     reads=[b_raw, b_vecs, b_rstd[ri]], writes=[b_raw])
                if prompt:
                    P.op("act", lambda e, c=c: e.copy(out=ckvT[:, c, t0:t0 + n], in_=raw[:, c, 0:n]), reads=[b_raw], pwrites=[b_ckvT])
                else:
                    P.op("act", lambda e, c=c: e.copy(out=ckvS[:, c, 0:n], in_=raw[:, c, 0:n]), reads=[b_raw], pwrites=[b_keyS])
            lat_dst, kr_dst, r0 = (lat_p[j], kr_p[j], t0) if prompt else (lat_s[j], kr_s[j], 0)
            for tt in range(n // 128):
                i = rot("tin")
                bk = bank()
                for c in range(2):
                    tr_block(ps[:, bk, c * 128:(c + 1) * 128], raw[:, c, tt * 128:(tt + 1) * 128], 128, bk, [b_raw], f32=True)
                tr_block(ps[:, bk, 256:288], raw[0:32, 2, tt * 128:(tt + 1) * 128], 32, bk, [b_raw], f32=True)
                P.op("act", lambda e, i=i, bk=bk: e.copy(out=tin[i][:, 0:288], in_=ps[:, bk, 0:288]), reads=[b_ps[bk]], writes=[b_tin[i]])
                if prompt:
                    P.op("dve", lambda e, bk=bk, tt=tt: e.tensor_copy(out=Knat[:, t0 // 128 + tt, :], in_=ps[:, bk, 0:256]),
                         reads=[b_ps[bk]], pwrites=[b_Knat])
                o1 = P.op("sp", lambda e, i=i, tt=tt: e.dma_start(out=lat_dst[r0 + tt * 128:r0 + (tt + 1) * 128, :], in_=tin[i][:, 0:256]),
                          reads=[b_tin[i]], dma=True)
                o2 = P.op("sp", lambda e, i=i, tt=tt: e.dma_start(out=kr_dst[r0 + tt * 128:r0 + (tt + 1) * 128, :], in_=tin[i][:, 256:288]),
                          reads=[b_tin[i]], dma=True)
                out_dmas.extend([o1, o2])

        def phaseB(W, mark, t0, n):
            raw, b_raw, cqT, b_cqT = W["raw"], W["b_raw"], W["cqT"], W["b_cqT"]
            st["ar"] = mark
            P.barrier()
            prompt = t0 < SEQ
            wuq = ar([128, 3, 768], BF16); b_wuq = B("wuq")
            wukv = ar([128, 2, 1024], BF16); b_wukv = B("wukv")
            wukT = ar([128, 8, 256], BF16); b_wukT = B("wukT")
            wqsw = ar([128, 3, 256], BF16); b_wqsw = B("wqsw")
            qn = ar([128, n], BF16); b_qn = B("qn")
            qpe = ar([128, n], BF16); b_qpe = B("qpe")
            qlat = ar([128, 2, n], BF16); b_qlat = B("qlat")
            assert (st["ar"] - mark) * 4 >= 8 * n * 4, "fo alias region too small"
            nstream = 4 if prompt else 2
            olTs = [ar([128, 2, 128], BF16) for _ in range(nstream)]; b_olTs = [B(f"olT{i_}") for i_ in range(nstream)]
            oT = ar([128, 8, n], BF16); b_oT = B("oT")
            scs = flash_scratch(nstream)
            prenorm(t0, n, g_pre)
            load_cs(W, t0, n)
            wb, bw, _ = load_w([(w_in, 0, 256)], D)
            for c in range(2):
                bk = bank()
                mm(ps[:, bk, 0:n], [(wb[:, k, c * 128:(c + 1) * 128], xn[:, k, 0:n]) for k in range(8)], [bw, b_xn], bk)
                P.op("act", lambda e, c=c, bk=bk: e.copy(out=raw[:, c, 0:n], in_=ps[:, bk, 0:n]), reads=[b_ps[bk]], pwrites=[b_raw])
            wb, bw, _ = load_w([(w_in, 256, 128)], D)
            bk = bank()
            mm(ps[:, bk, 0:n], [(wb[:, k, 0:128], xn[:, k, 0:n]) for k in range(8)], [bw, b_xn], bk)
            P.op("act", lambda e, bk=bk: e.copy(out=raw[:, 2, 0:n], in_=ps[:, bk, 0:n]), reads=[b_ps[bk]], pwrites=[b_raw])
            bk = stats_bc([(raw[:, c, 0:n], 128, [b_raw]) for c in range(3)], n)
            ri = rstd_from(bk, n, 384, RMS_EPS)
            for c in range(3):
                P.op("dve", lambda e, c=c: e.scalar_tensor_tensor(out=cqT[:, c, 0:n], in0=raw[:, c, 0:n], scalar=vec(R_QN + j, c),
                                                                   in1=rstd[ri][:, 0:n], op0=ALU.mult, op1=ALU.mult),
                     reads=[b_raw, b_vecs, b_rstd[ri]], pwrites=[b_cqT])

            def load_head_weights(hb, need_q=True):
                load_w_to([(mla_w_ukv[j], hb * 1024, 1024)], 256, wukv[:, :, :], b_wukv)
                if not need_q:
                    return
                for q2 in range(2):
                    load_w_to([(mla_w_uq[j], hb * 768 + q2 * 384, 384)], 384, wuq[:, :, q2 * 384:(q2 + 1) * 384], b_wuq)
                wuq4 = lambda c: wuq[:, c, :].rearrange("p (h d) -> p h d", h=8)
                wqsw4 = lambda c: wqsw[:, c, :].rearrange("p (h d) -> p h d", h=8)
                for c in range(3):
                    P.op("pool", lambda e, c=c: e.tensor_scalar(out=wqsw4(c)[:, :, 0:16], in0=wuq4(c)[:, :, 80:96], scalar1=-1.0, scalar2=None, op0=ALU.mult),
                         reads=[b_wuq], pwrites=[b_wqsw])
                    P.op("pool", lambda e, c=c: e.tensor_copy(out=wqsw4(c)[:, :, 16:32], in_=wuq4(c)[:, :, 64:80]), reads=[b_wuq], pwrites=[b_wqsw])
                for hl in range(8):
                    bk = bank()
                    for lc in range(2):
                        tr_block(ps[0:64, bk, lc * 128:(lc + 1) * 128], wukv[:, lc, hl * 128:hl * 128 + 64], 128, bk, [b_wukv])
                    P.op("act", lambda e, hl=hl, bk=bk: e.copy(out=wukT[0:64, hl, :], in_=ps[0:64, bk, 0:256]), reads=[b_ps[bk]], pwrites=[b_wukT])

            if not prompt:
                qlat_all = ar([128, 2, 2048], BF16); b_qla = B("qlat_all")
                qpe_all = ar([128, 2048], BF16); b_qpa = B("qpe_all")
                olat_all = ar([128, 2, 2048], BF16); b_ola = B("olat_all")
            qsets = [dict(qn=qn, b_qn=b_qn, qpe=qpe, b_qpe=b_qpe, qlat=qlat, b_qlat=b_qlat)]
            if prompt:
                P.barrier()
                rflat = raw[:, :, :].rearrange("p a b -> p (a b)").bitcast(BF16)
                qsets.append(dict(qn=rflat[:, 0:n], b_qn=B("qn2"), qpe=rflat[:, n:2 * n], b_qpe=B("qpe2"),
                                  qlat=rflat[:, 2 * n:4 * n].rearrange("p (a b) -> p a b", a=2), b_qlat=B("qlat2")))

            def qproj_gen(hd, Q):
                hl = hd % 8
                bn = bank(); bx = bank(); bs_ = bank()
                mm(ps[0:64, bn, 0:n], [(wuq[:, c, hl * 96:hl * 96 + 64], cqT[:, c, 0:n]) for c in range(3)], [b_wuq, b_cqT], bn)
                mm(ps[0:32, bx, 0:n], [(wuq[:, c, hl * 96 + 64:hl * 96 + 96], cqT[:, c, 0:n]) for c in range(3)], [b_wuq, b_cqT], bx)
                mm(ps[0:32, bs_, 0:n], [(wqsw[:, c, hl * 32:(hl + 1) * 32], cqT[:, c, 0:n]) for c in range(3)], [b_wqsw, b_cqT], bs_)
                P.op("act", lambda e: e.copy(out=Q["qn"][0:64, 0:n], in_=ps[0:64, bn, 0:n]), reads=[b_ps[bn]], writes=[Q["b_qn"]])
                if prompt:
                    rope_apply(W, n, bx, bs_, Q["qpe"][0:32, 0:n], Q["b_qpe"])
                else:
                    rope_apply(W, n, bx, bs_, qpe_all[0:32, hd * 128:(hd + 1) * 128], b_qpa)
                yield
                for lc in range(2):
                    bk = bank()
                    mm(ps[:, bk, 0:n], [(wukT[0:64, hl, lc * 128:(lc + 1) * 128], Q["qn"][0:64, 0:n])], [b_wukT, Q["b_qn"]], bk)
                    if prompt:
                        P.op("act", lambda e, lc=lc, bk=bk: e.copy(out=Q["qlat"][:, lc, 0:n], in_=ps[:, bk, 0:n]), reads=[b_ps[bk]], pwrites=[Q["b_qlat"]])
                    else:
                        P.op("act", lambda e, lc=lc, bk=bk: e.copy(out=qlat_all[:, lc, hd * 128:(hd + 1) * 128], in_=ps[:, bk, 0:n]),
                             reads=[b_ps[bk]], pwrites=[b_qla])
                yield

            for hd in range(16):
                hb, hl = hd // 8, hd % 8
                if hl == 0:
                    load_head_weights(hb)
                Q = qsets[hd % len(qsets)]
                if hl == 0 or not prompt:
                    run_gens([qproj_gen(hd, Q)])
                if not prompt:
                    continue
                gens = []
                for qi in range(n // 128):
                    T = t0 // 128 + qi
                    nkeys = (T + 1) * 128
                    chunks = []
                    for c0 in range(0, nkeys, 512):
                        nk = min(512, nkeys - c0)
                        ch = dict(nk=nk, KT=[(ckvT[:, 0, c0:c0 + nk], [b_ckvT]), (ckvT[:, 1, c0:c0 + nk], [b_ckvT]), (kpeT[0:32, c0:c0 + nk], [b_kpeT])],
                                  V=[(Knat[:, c0 // 128 + kb, :], 128, [b_Knat]) for kb in range(nk // 128)])
                        if c0 + nk == nkeys:
                            ch["mask"] = (nk - 128, maskb[:, 0:128], 128)
                        chunks.append(ch)
                    qs = slice(qi * 128, (qi + 1) * 128)

                    def dst(view, bkf, hd=hd, hl=hl, qi=qi):
                        olT, b_olT = olTs[qi % nstream], b_olTs[qi % nstream]
                        P.op("act", lambda e: e.copy(out=olT[:, :, :], in_=view), reads=[b_ps[bkf]], writes=[b_olT])
                        yield
                        bo = bank()
                        po = (hd % 2) * 64
                        mmx([(ps[po:po + 64, bo, 0:128], wukv[:, lc, hl * 128 + 64:hl * 128 + 128], olT[:, lc, :]) for lc in range(2)], [b_wukv, b_olT], bo)
                        P.op("act", lambda e: e.copy(out=oT[po:po + 64, hd // 2, qi * 128:(qi + 1) * 128], in_=ps[po:po + 64, bo, 0:128]),
                             reads=[b_ps[bo]], pwrites=[b_oT])
                    gens.append(flash_gen(128, [(Q["qlat"][:, 0, qs], [Q["b_qlat"]]), (Q["qlat"][:, 1, qs], [Q["b_qlat"]]), (Q["qpe"][0:32, qs], [Q["b_qpe"]])],
                                          chunks, 256, scale, scs[qi % nstream], dst))
                if hl != 7:
                    gens.append(qproj_gen(hd + 1, qsets[(hd + 1) % len(qsets)]))
                run_gens(gens)
            if not prompt:
                sample_attn(scs, qlat_all, b_qla, qpe_all, b_qpa, olat_all, b_ola)
                for hb in range(2):
                    load_head_weights(hb, need_q=False)
                    for hl in range(8):
                        hd = hb * 8 + hl
                        bo = bank()
                        po = (hd % 2) * 64
                        mmx([(ps[po:po + 64, bo, 0:128], wukv[:, lc, hl * 128 + 64:hl * 128 + 128], olat_all[:, lc, hd * 128:(hd + 1) * 128]) for lc in range(2)],
                            [b_wukv, b_ola], bo)
                        P.op("act", lambda e, hd=hd, bo=bo, po=po: e.copy(out=oT[po:po + 64, hd // 2, 0:128], in_=ps[po:po + 64, bo, 0:128]),
                             reads=[b_ps[bo]], pwrites=[b_oT])
            st_save = st["ar"]
            P.barrier()
            st["ar"] = mark
            fo = ar([128, 8, n]); b_fo = B("fo")
            st["ar"] = st_save
            out_linear(mla_w_o[j], D, lambda k: oT[:, k, 0:n], [b_oT], n, fo, b_fo)
            postnorm(t0, n, g_post, 1.0, fo, b_fo)

        def sample_attn(scs, qlat_all, b_qla, qpe_all, b_qpa, olat_all, b_ola):
            pti = [ar([128, 16], I32) for _ in range(2)]; ptf = [ar([128, 16]) for _ in range(2)]; idx = [ar([128, 16], I32) for _ in range(2)]
            b_pti = [B("pti0"), B("pti1")]; b_ptf = [B("ptf0"), B("ptf1")]; b_idx = [B("idx0"), B("idx1")]
            qs_lat = ar([128, 2, 128], BF16); b_qsl = B("qs_lat")
            qs_pe = ar([128, 128], BF16); b_qsp = B("qs_pe")
            stgL = [ar([128, 4, 256]) for _ in range(2)]; stgR = [ar([128, 4, 32]) for _ in range(2)]; b_stg = [B("stg0"), B("stg1")]
            KbL = [ar([128, 4, 256], BF16) for _ in range(2)]; KbR = [ar([128, 4, 32], BF16) for _ in range(2)]; b_Kb = [B("Kb0"), B("Kb1")]
            KTc = [ar([128, 2, 512], BF16) for _ in range(2)]; b_KTc = [B("KTc0"), B("KTc1")]
            KrT = [ar([128, 512], BF16) for _ in range(2)]; b_KrT = [B("KrT0"), B("KrT1")]
            vnew = ar([128, 256], BF16); b_vnew = B("vnew")
            latv = lat_pool[j].rearrange("(g r) d -> g (r d)", r=4)
            krv = kr_pool[j].rearrange("(g r) d -> g (r d)", r=4)
            cnt = [0]
            for s in range(16):
                si = s % 2
                sl = slice(s * 8, (s + 1) * 8)
                for pg in range(4):
                    P.op("sp", lambda e, s=s, si=si, pg=pg: e.dma_start(
                        out=pti[si][pg * 32:(pg + 1) * 32, :],
                        in_=pt_in[s, :].rearrange("(cc pg) -> pg cc", pg=4)[pg].partition_broadcast(32), allow_slow_non_contiguous=True),
                        pwrites=[b_pti[si]], dma=True)
                P.op("dve", lambda e, si=si: e.tensor_copy(out=ptf[si][:, :], in_=pti[si][:, :]), reads=[b_pti[si]], writes=[b_ptf[si]])
                P.op("dve", lambda e, si=si: e.tensor_scalar(out=ptf[si][:, :], in0=ptf[si][:, :], scalar1=32.0, scalar2=iot[:, 1:2],
                                                             op0=ALU.mult, op1=ALU.add),
                     reads=[b_ptf[si], b_const], writes=[b_ptf[si]])
                P.op("dve", lambda e, si=si: e.tensor_copy(out=idx[si][:, :], in_=ptf[si][:, :]), reads=[b_ptf[si]], writes=[b_idx[si]])
                for lc in range(2):
                    P.op("dve", lambda e, lc=lc, sl=sl: e.tensor_copy(out=qs_lat[:, lc, :].rearrange("p (h t) -> p h t", h=16),
                                                                      in_=qlat_all[:, lc, :].rearrange("p (h t) -> p h t", h=16)[:, :, sl]),
                         reads=[b_qla], pwrites=[b_qsl])
                P.op("dve", lambda e, sl=sl: e.tensor_copy(out=qs_pe[0:32, :].rearrange("p (h t) -> p h t", h=16),
                                                           in_=qpe_all[0:32, :].rearrange("p (h t) -> p h t", h=16)[:, :, sl]),
                     reads=[b_qpa], writes=[b_qsp])
                bk = bank()
                for lc in range(2):
                    tr_block(ps[0:8, bk, lc * 128:(lc + 1) * 128], ckvS[:, lc, sl], 128, bk, [b_keyS])
                P.op("act", lambda e, bk=bk: e.copy(out=vnew[0:8, :], in_=ps[0:8, bk, 0:256]), reads=[b_ps[bk]], writes=[b_vnew])

                def make_chunk(cc, s=s, si=si):
                    bi = cnt[0] % 2
                    cnt[0] += 1
                    P.op("pool", lambda e, bi=bi, cc=cc: e.indirect_dma_start(
                        out=stgL[bi][:, :, :].rearrange("p r d -> p (r d)"), out_offset=None, in_=latv,
                        in_offset=bass.IndirectOffsetOnAxis(ap=idx[si][:, cc:cc + 1], axis=0)),
                        reads=[b_idx[si]], pwrites=[b_stg[bi]], dma=True)
                    P.op("pool", lambda e, bi=bi, cc=cc: e.indirect_dma_start(
                        out=stgR[bi][:, :, :].rearrange("p r d -> p (r d)"), out_offset=None, in_=krv,
                        in_offset=bass.IndirectOffsetOnAxis(ap=idx[si][:, cc:cc + 1], axis=0)),
                        reads=[b_idx[si]], pwrites=[b_stg[bi]], dma=True)
                    cast(KbL[bi][:, :, :], stgL[bi][:, :, :], [b_stg[bi]], dict(writes=[b_Kb[bi]]))
                    P.op("pool", lambda e, bi=bi: e.tensor_copy(out=KbR[bi][:, :, :], in_=stgR[bi][:, :, :]), reads=[b_stg[bi]], pwrites=[b_Kb[bi]])
                    for lc in range(2):
                        bk = bank()
                        for r in range(4):
                            tr_block(ps[:, bk, r * 128:(r + 1) * 128], KbL[bi][:, r, lc * 128:(lc + 1) * 128], 128, bk, [b_Kb[bi]])
                        if lc:
                            P.op("act", lambda e, bi=bi, lc=lc, bk=bk: e.copy(out=KTc[bi][:, lc, :], in_=ps[:, bk, :]), reads=[b_ps[bk]], pwrites=[b_KTc[bi]])
                        else:
                            P.op("dve", lambda e, bi=bi, lc=lc, bk=bk: e.tensor_copy(out=KTc[bi][:, lc, :], in_=ps[:, bk, :]), reads=[b_ps[bk]], pwrites=[b_KTc[bi]])
                    bk = bank()
                    for r in range(4):
                        tr_block(ps[0:32, bk, r * 128:(r + 1) * 128], KbR[bi][:, r, :], 128, bk, [b_Kb[bi]])
                    P.op("act", lambda e, bi=bi, bk=bk: e.copy(out=KrT[bi][0:32, :], in_=ps[0:32, bk, :]), reads=[b_ps[bk]], writes=[b_KrT[bi]])
                    return dict(nk=512, KT=[(KTc[bi][:, 0, :], [b_KTc[bi]]), (KTc[bi][:, 1, :], [b_KTc[bi]]), (KrT[bi][0:32, :], [b_KrT[bi]])],
                                V=[(KbL[bi][:, r, :], 128, [b_Kb[bi]]) for r in range(4)])
                chunks = [(lambda cc=cc: make_chunk(cc)) for cc in range(16)]
                chunks.append(dict(nk=8, KT=[(ckvS[:, 0, sl], [b_keyS]), (ckvS[:, 1, sl], [b_keyS]), (kpeS[0:32, sl], [b_keyS])],
                                   V=[(vnew[0:8, :], 8, [b_vnew])], mask=(0, maskb[:, 128:136], 8)))

                def dst(view, bkf, s=s):
                    for lc in range(2):
                        P.op("act", lambda e, lc=lc: e.copy(out=olat_all[:, lc, :].rearrange("p (h t) -> p h t", h=16)[:, :, s * 8:(s + 1) * 8],
                                                            in_=view[:, lc, :].rearrange("p (h t) -> p h t", h=16)),
                             reads=[b_ps[bkf]], pwrites=[b_ola])
                flash(128, [(qs_lat[:, 0, :], [b_qsl]), (qs_lat[:, 1, :], [b_qsl]), (qs_pe[0:32, :], [b_qsp])], chunks, 256, scale, scs[s % 2], dst)

        W = alloc_work(512)
        mark = st["ar"]
        for (t0, n) in GROUPS:
            phaseA(W, t0, n)
        for (t0, n) in GROUPS:
            if t0 < SEQ:
                phaseB(W, mark, t0, n)
        P.barrier()
        st["ar"] = mark0
        W2 = alloc_work(128)
        mark2 = st["ar"]
        phaseB(W2, mark2, SEQ, NS)

    load_tokens(xp, SEQ, 0)
    load_tokens(xs, NS, SEQ)
    subs = []
    for l in range(4):
        subs.append(lambda l=l: ffn(l, 0, 0, 1))
        subs.append((lambda l=l: convmod(l)) if l % 2 == 0 else (lambda l=l: mla(l)))
        subs.append(lambda l=l: xattn(l))
        subs.append(lambda l=l: ffn(l, 1, 6, 7))
    for f_ in subs[:nsub]:
        f_()

    P.barrier()
    hsrc = lambda tokbase: [((lambda t, c=c: h[:, c, tokbase + t:tokbase + t + 128]), 128, b_h[c]) for c in range(8)]
    store_rows(y_p, hsrc(0), SEQ)
    store_rows(y_s, hsrc(SEQ), NS)
    fin = P.op("sp", None)
    fin.deps.update(out_dmas)
    P.emit()
    es.close()
    return nc


def _host_tables():
    f32 = np.float32
    consts = np.zeros((128, 512), f32)
    consts[:, 0:128] = np.eye(128, dtype=f32)
    q = np.arange(128)[:, None]; k = np.arange(128)[None, :]
    consts[:, 128:256] = np.where(k <= q, 0.0, -30000.0).astype(f32)
    t = (np.arange(128) % 8)[:, None]; tp = np.arange(8)[None, :]
    consts[:, 256:264] = np.where(tp <= t, 0.0, -30000.0).astype(f32)
    pos = np.concatenate([np.arange(SEQ), np.tile(8192 + np.arange(8), 16)]).astype(f32)
    inv_freq = (f32(10000.0) ** (-(np.arange(0, 32, 2, dtype=f32)) / f32(32))).astype(f32)
    ang = (pos[:, None] * inv_freq[None, :]).astype(f32)
    cos = np.cos(ang).astype(f32).T; sin = np.sin(ang).astype(f32).T
    rope = np.stack([np.concatenate([cos, cos], 0), np.concatenate([sin, sin], 0)], 0).astype(f32)
    iota = np.stack([np.arange(128), np.arange(128) % 32], 1).astype(f32)
    return consts, np.ascontiguousarray(rope), iota


def make_in_maps(inp, n_cores=8):
    f32 = np.float32
    A = lambda k: np.asarray(inp[k])
    vtab = np.zeros((128, D), f32)
    vtab[R_GAIN:R_GAIN + 32] = A("norm_gain").reshape(32, D)
    vtab[R_BPW1:R_BPW1 + 4] = A("conv_b_pw1").reshape(4, D)
    vtab[R_BDW:R_BDW + 2] = A("conv_b_dw"); vtab[R_LNG:R_LNG + 2] = A("conv_ln_g"); vtab[R_LNB:R_LNB + 2] = A("conv_ln_b")
    vtab[R_BPW2:R_BPW2 + 2] = A("conv_b_pw2"); vtab[R_MEMN:R_MEMN + 4] = A("xa_mem_norm")
    vtab[R_WDW:R_WDW + 62] = A("conv_w_dw").reshape(62, D)
    vtab[R_QN:R_QN + 2, 0:384] = A("mla_q_norm"); vtab[R_KVN:R_KVN + 2, 0:256] = A("mla_kv_norm")
    consts, rope, iota = _host_tables()
    C = np.ascontiguousarray
    n_pool = A("cache_mla_latent_l1").shape[0]
    shared = dict(vtab=vtab, consts=consts, rope=rope, iota=iota,
                  ffn_w_in=C(A("ffn_w_in"), dtype=f32), ffn_w_out=C(A("ffn_w_out"), dtype=f32),
                  lat1=A("cache_mla_latent_l1").reshape(n_pool * 128, 256), lat3=A("cache_mla_latent_l3").reshape(n_pool * 128, 256),
                  kr1=A("cache_mla_krope_l1").reshape(n_pool * 128, 32), kr3=A("cache_mla_krope_l3").reshape(n_pool * 128, 32),
                  conv_w_pw1=C(A("conv_w_pw1")), conv_w_pw2=C(A("conv_w_pw2")),
                  mla_w_in=C(A("mla_w_in")), mla_w_uq=A("mla_w_uq").reshape(2, 384, 1536), mla_w_ukv=A("mla_w_ukv").reshape(2, 256, 2048),
                  mla_w_o=C(A("mla_w_o")), xa_w_q=C(A("xa_w_q")), xa_w_kv=C(A("xa_w_kv")), xa_w_o=C(A("xa_w_o")))
    maps = []
    for c in range(n_cores):
        s0 = 16 * c
        m = dict(shared)
        m.update(xp=C(A("x_prompt")[c]), xs=C(A("x_sample")[s0:s0 + 16].reshape(NS, D)),
                 cs0=C(A("state_conv_l0")[s0:s0 + 16]), cs2=C(A("state_conv_l2")[s0:s0 + 16]),
                 memk=C(A("cache_mem_k")[:, s0:s0 + 16].reshape(4, 16, 256, 512)), memv=C(A("cache_mem_v")[:, s0:s0 + 16].reshape(4, 16, 256, 512)),
                 pt=C(A("page_table")[s0:s0 + 16].astype(np.int32)), memp=C(A("mem_prompt")[c]))
        maps.append(m)
    return maps, n_pool


def gather_outputs(r, n):
    f32 = np.float32
    g = lambda k: [np.asarray(r[c][k], f32) for c in range(n)]
    y_prompt = np.stack(g("y_p"), 0)
    y_sample = np.concatenate(g("y_s"), 0).reshape(16 * n, 8, D)
    conv0_p = np.stack(g("conv0_p"), 0); conv2_p = np.stack(g("conv2_p"), 0)
    lat1_p = np.stack(g("lat1_p"), 0); kr1_p = np.stack(g("kr1_p"), 0)
    lat3_p = np.stack(g("lat3_p"), 0); kr3_p = np.stack(g("kr3_p"), 0)
    memk = np.stack(g("memk_p"), 1).reshape(4, n, 256, 4, 128); memv = np.stack(g("memv_p"), 1).reshape(4, n, 256, 4, 128)
    conv0_s = np.concatenate(g("conv0_s"), 0); conv2_s = np.concatenate(g("conv2_s"), 0)
    lat1_s = np.concatenate(g("lat1_s"), 0).reshape(16 * n, 8, 256); kr1_s = np.concatenate(g("kr1_s"), 0).reshape(16 * n, 8, 32)
    lat3_s = np.concatenate(g("lat3_s"), 0).reshape(16 * n, 8, 256); kr3_s = np.concatenate(g("kr3_s"), 0).reshape(16 * n, 8, 32)
    return (y_prompt, y_sample, conv0_p, conv2_p, lat1_p, kr1_p, lat3_p, kr3_p, memk, memv,
            conv0_s, conv2_s, lat1_s, kr1_s, lat3_s, kr3_s)


def kernel(**inp):
    n = 8
    maps, n_pool = make_in_maps(inp, n)
    nc = build(nsub=16, n_pool=n_pool)
    res = run_bass_kernel_spmd(nc, maps, core_ids=list(range(n)))
    return gather_outputs(res.results, n)
```
